# Optimizing a Trainium2 kernel written in Bass

```python
import math
import jax, jax.numpy as jnp
from jax import lax
import numpy as np

D_MODEL = 2048
BATCH = 2
SEQ = 16384
DEPTH = 1

GRID_W = 64
PLE_DIM = 256
MIX_W = D_MODEL
RWKV_W = MIX_W // 2
NAT_W = MIX_W - RWKV_W
HEAD_SIZE = 64
RWKV_HEADS = RWKV_W // HEAD_SIZE
NAT_HEADS = NAT_W // HEAD_SIZE
DECAY_LORA = 64
ICLR_LORA = 64
N_DIR = 2
NAT_KH = 8
NAT_KW = 16
NORM_EPS = 1e-6
LNX_EPS = 64e-5
DECAY_SCALE = math.exp(-0.5)

O_R = 0
O_K = RWKV_W
O_V = 2 * RWKV_W
O_WD = 3 * RWKV_W
O_AD = O_WD + N_DIR * DECAY_LORA
SHIFT_W = O_AD + N_DIR * ICLR_LORA
O_G_RWKV = SHIFT_W
O_NAT = SHIFT_W + RWKV_W
IN_W = O_NAT + 4 * NAT_W

kernel_name = "hybrid_rwkv7_natten2d_bidir_block"


def rms_norm(x, g):
    xf = x.astype(jnp.float32)
    y = xf * lax.rsqrt(jnp.mean(xf * xf, axis=-1, keepdims=True) + NORM_EPS)
    return (y * g.astype(jnp.float32)).astype(x.dtype)


def centred_shift_mix(z, mu_prev, mu_next):
    prev = jnp.pad(z[:, :-1], ((0, 0), (1, 0), (0, 0)))
    nxt = jnp.pad(z[:, 1:], ((0, 0), (0, 1), (0, 0)))
    return z + mu_prev * (prev - z) + mu_next * (nxt - z)


def wkv7_scan(r, w, k, v, a_vec, b_vec, reverse):
    bsz, _, nh, n = r.shape
    xs = tuple(jnp.moveaxis(t, 1, 0) for t in (r, w, k, v, a_vec, b_vec))

    def step(state, inp):
        r_t, w_t, k_t, v_t, a_t, b_t = inp
        sa = jnp.einsum('bhvk,bhk->bhv', state, a_t)
        state = (state * w_t[:, :, None, :]
                 + sa[..., None] * b_t[:, :, None, :]
                 + v_t[..., None] * k_t[:, :, None, :])
        y = jnp.einsum('bhvk,bhk->bhv', state, r_t)
        return state, y

    s0 = jnp.zeros((bsz, nh, n, n), jnp.float32)
    _, ys = lax.scan(step, s0, xs, reverse=reverse)
    return jnp.moveaxis(ys, 0, 1)


def rwkv7_branch(zs, gate, w0, w2, a0, a2, k_k, k_a, r_k, lnx_w, lnx_b):
    bsz, t_len, _ = zs.shape
    zf = zs.astype(jnp.float32)
    hd = (bsz, t_len, RWKV_HEADS, HEAD_SIZE)
    r = zf[..., O_R:O_K]
    k = zf[..., O_K:O_V]
    v = zf[..., O_V:O_WD]
    wd = zf[..., O_WD:O_AD].reshape(bsz, t_len, N_DIR, DECAY_LORA)
    ad = zf[..., O_AD:SHIFT_W].reshape(bsz, t_len, N_DIR, ICLR_LORA)
    f32 = lambda a: a.astype(jnp.float32)
    decay = jnp.exp(-DECAY_SCALE * jax.nn.sigmoid(
        f32(w0) + jnp.einsum('btdr,drc->btdc', jnp.tanh(wd), f32(w2))))
    iclr = jax.nn.sigmoid(f32(a0) + jnp.einsum('btdr,drc->btdc', ad, f32(a2)))
    kk = (k * f32(k_k)).reshape(hd)
    kk = kk / jnp.maximum(jnp.sqrt(jnp.sum(kk * kk, axis=-1, keepdims=True)), 1e-12)
    k_dir = k[:, :, None, :] * (1.0 + (iclr - 1.0) * f32(k_a))
    rh, vh = r.reshape(hd), v.reshape(hd)
    ys = []
    for d in range(N_DIR):
        ys.append(wkv7_scan(rh, decay[:, :, d].reshape(hd), k_dir[:, :, d].reshape(hd), vh,
                            -kk, kk * iclr[:, :, d].reshape(hd), reverse=(d == 1)))
    y = ys[0] + ys[1]
    mu = jnp.mean(y, axis=-1, keepdims=True)
    var = jnp.mean(jnp.square(y - mu), axis=-1, keepdims=True)
    y = (y - mu) * lax.rsqrt(var + LNX_EPS)
    y = y * f32(lnx_w).reshape(RWKV_HEADS, HEAD_SIZE) + f32(lnx_b).reshape(RWKV_HEADS, HEAD_SIZE)
    bonus = jnp.sum(rh * k.reshape(hd) * f32(r_k), axis=-1, keepdims=True) * vh
    out = (y + bonus).reshape(bsz, t_len, RWKV_W) * jax.nn.silu(gate.astype(jnp.float32))
    return out.astype(zs.dtype)


def nat2d_branch(q, k, v, rpb):
    bsz, t_len, _ = q.shape
    rows = t_len // GRID_W
    kh = min(NAT_KH, rows)
    scale = HEAD_SIZE ** -0.5

    def to_grid(a):
        return a.reshape(bsz, rows, GRID_W, NAT_HEADS, HEAD_SIZE).transpose(0, 3, 1, 2, 4)

    qg, kg, vg = to_grid(q), to_grid(k), to_grid(v)
    row_start = jnp.clip(jnp.arange(rows) - kh // 2, 0, rows - kh)
    cols = jnp.arange(GRID_W)
    col_idx = jnp.clip(cols - NAT_KW // 2, 0, GRID_W - NAT_KW)[:, None] + jnp.arange(NAT_KW)[None, :]
    bias_c = col_idx - cols[:, None] + (NAT_KW - 1)

    def one_row(args):
        r, q_row = args
        rs = row_start[r]
        k_rows = lax.dynamic_slice_in_dim(kg, rs, kh, axis=2)
        v_rows = lax.dynamic_slice_in_dim(vg, rs, kh, axis=2)
        k_nb = k_rows[:, :, :, col_idx]
        v_nb = v_rows[:, :, :, col_idx]
        s = jnp.einsum('bhcd,bhicjd->bhcij', q_row, k_nb).astype(jnp.float32) * scale
        bias_r = rs + jnp.arange(kh) - r + (NAT_KH - 1)
        bias = rpb[:, bias_r[None, :, None], bias_c[:, None, :]]
        s = s + bias.astype(jnp.float32)[None]
        prob = jax.nn.softmax(s.reshape(bsz, NAT_HEADS, GRID_W, kh * NAT_KW), axis=-1)
        prob = prob.reshape(bsz, NAT_HEADS, GRID_W, kh, NAT_KW).astype(q.dtype)
        return jnp.einsum('bhcij,bhicjd->bhcd', prob, v_nb)

    out = lax.map(one_row, (jnp.arange(rows), jnp.moveaxis(qg, 2, 0)))
    return out.transpose(1, 0, 3, 2, 4).reshape(bsz, t_len, NAT_W)


def setup_inputs(seed: int = 0) -> dict:
    key = jax.random.key(seed)
    ks = jax.random.split(key, 24)
    nrm = lambda k, s: jax.random.normal(k, s, jnp.float32)
    return {
        "x": nrm(ks[0], (BATCH, SEQ, D_MODEL)),
        "p": nrm(ks[1], (DEPTH, BATCH, SEQ, PLE_DIM)),
        "norm_mix_g": 1.0 + 0.02 * nrm(ks[2], (DEPTH, D_MODEL)),
        "w_in": nrm(ks[3], (DEPTH, D_MODEL, IN_W)) * D_MODEL ** -0.5,
        "shift_mu_prev": jax.random.uniform(ks[4], (DEPTH, SHIFT_W), jnp.float32, 0.0, 0.5),
        "shift_mu_next": jax.random.uniform(ks[5], (DEPTH, SHIFT_W), jnp.float32, 0.0, 0.5),
        "decay_w0": jax.random.uniform(ks[6], (DEPTH, N_DIR, RWKV_W), jnp.float32, -6.0, 1.0),
        "decay_w2": nrm(ks[7], (DEPTH, N_DIR, DECAY_LORA, RWKV_W)) * 0.5 * DECAY_LORA ** -0.5,
        "iclr_a0": 0.5 * nrm(ks[8], (DEPTH, N_DIR, RWKV_W)),
        "iclr_a2": nrm(ks[9], (DEPTH, N_DIR, ICLR_LORA, RWKV_W)) * 0.5 * ICLR_LORA ** -0.5,
        "k_k": 0.85 + 0.05 * nrm(ks[10], (DEPTH, RWKV_W)),
        "k_a": 1.0 + 0.05 * nrm(ks[11], (DEPTH, RWKV_W)),
        "r_k": 0.1 * nrm(ks[12], (DEPTH, RWKV_HEADS, HEAD_SIZE)),
        "lnx_w": 1.0 + 0.02 * nrm(ks[13], (DEPTH, RWKV_W)),
        "lnx_b": 0.02 * nrm(ks[14], (DEPTH, RWKV_W)),
        "nat_rpb": 0.1 * nrm(ks[15], (DEPTH, NAT_HEADS, 2 * NAT_KH - 1, 2 * NAT_KW - 1)),
        "w_out": nrm(ks[16], (DEPTH, MIX_W, D_MODEL)) * MIX_W ** -0.5,
        "ple_norm_g": 1.0 + 0.02 * nrm(ks[17], (DEPTH, D_MODEL)),
        "w_ple_gate": nrm(ks[18], (DEPTH, D_MODEL, D_MODEL)) * D_MODEL ** -0.5,
        "w_ple_proj": nrm(ks[19], (DEPTH, PLE_DIM, D_MODEL)) * PLE_DIM ** -0.5,
        "final_norm_g": 1.0 + 0.02 * nrm(ks[20], (D_MODEL,)),
    }


def reference(x, p, norm_mix_g, w_in, shift_mu_prev, shift_mu_next, decay_w0, decay_w2,
              iclr_a0, iclr_a2, k_k, k_a, r_k, lnx_w, lnx_b, nat_rpb, w_out,
              ple_norm_g, w_ple_gate, w_ple_proj, final_norm_g):
    h = x
    for i in range(DEPTH):
        hn = rms_norm(h, norm_mix_g[i])
        z = hn @ w_in[i]
        zs = centred_shift_mix(z[..., :SHIFT_W], shift_mu_prev[i], shift_mu_next[i])
        y_a = rwkv7_branch(zs, z[..., O_G_RWKV:O_NAT], decay_w0[i], decay_w2[i],
                           iclr_a0[i], iclr_a2[i], k_k[i], k_a[i], r_k[i], lnx_w[i], lnx_b[i])
        qn = z[..., O_NAT:O_NAT + NAT_W]
        kn = z[..., O_NAT + NAT_W:O_NAT + 2 * NAT_W]
        vn = z[..., O_NAT + 2 * NAT_W:O_NAT + 3 * NAT_W]
        gn = z[..., O_NAT + 3 * NAT_W:O_NAT + 4 * NAT_W]
        y_b = nat2d_branch(qn, kn, vn, nat_rpb[i]) * jax.nn.silu(gn)
        h = h + jnp.concatenate([y_a, y_b], axis=-1) @ w_out[i]
        ple_gate = jax.nn.sigmoid(rms_norm(h, ple_norm_g[i]) @ w_ple_gate[i])
        h = h + (p[i] @ w_ple_proj[i]) * ple_gate
    return rms_norm(h, final_norm_g)
```

```python
import contextlib
import os
import math
import numpy as np
import concourse.bass as bass
import concourse.mybir as mybir
from concourse.bass_utils import run_bass_kernel_spmd

F32 = mybir.dt.float32
BF16 = mybir.dt.bfloat16
ALU = mybir.AluOpType
AF = mybir.ActivationFunctionType
AX = mybir.AxisListType

D = 2048
KC = 16
IN_W = 8448
NCC = 66
C = 64
TT = 256
TP = 512
NCH = TT // C
DSC = math.exp(-0.5)
NEG = -30000.0

ENGS = ("pe", "act", "dve", "pool", "sp")
STRICT_SAME_ENGINE = os.environ.get('KSTRICT', '1') == '1'
N_DMA_SEMS = 24


class Buf:
    __slots__ = ("name", "w", "r")

    def __init__(self, name):
        self.name = name
        self.w = None
        self.r = []


class Op:
    __slots__ = ("eng", "fn", "deps", "idx", "is_dma", "sig", "ev")

    def __init__(self, eng, fn, is_dma):
        self.eng = eng
        self.fn = fn
        self.deps = []
        self.is_dma = is_dma
        self.sig = False
        self.ev = None


class Sched:
    def __init__(self, nc):
        self.nc = nc
        self.ops = []
        self.last = {e: None for e in ENGS}

    def op(self, eng, fn, reads=(), writes=(), dma=False):
        o = Op(eng, fn, dma)
        deps = set()
        for b in reads:
            if b.w is not None:
                deps.add(b.w)
        for b in writes:
            if b.w is not None:
                deps.add(b.w)
            for r in b.r:
                deps.add(r)
        for b in reads:
            if not dma:
                b.r = [x for x in b.r if x.is_dma or x.eng != eng]
            b.r.append(o)
        for b in writes:
            b.w = o
            b.r = []
        deps.discard(o)
        o.deps = list(deps)
        o.idx = len(self.ops)
        self.ops.append(o)
        self.last[eng] = o
        return o

    def barrier(self):
        lasts = [o for o in self.last.values() if o is not None]
        for e in ENGS:
            o = Op(e, None, False)
            o.deps = list(lasts)
            o.idx = len(self.ops)
            self.ops.append(o)
            self.last[e] = o

    def finish(self, out_ops):
        o = Op("sp", None, False)
        o.deps = list(out_ops)
        o.idx = len(self.ops)
        self.ops.append(o)

    def emit(self, st):
        nc = self.nc
        engs = {"pe": nc.tensor, "act": nc.scalar, "dve": nc.vector, "pool": nc.gpsimd, "sp": nc.sync}
        sem = {e: st.enter_context(nc.semaphore("sem_" + e)) for e in ENGS}
        dsems = [st.enter_context(nc.semaphore("dsem%d" % i)) for i in range(N_DMA_SEMS)]
        for o in self.ops:
            for d in o.deps:
                if d.is_dma:
                    continue
                if d.eng != o.eng or o.is_dma:
                    d.sig = True
                elif STRICT_SAME_ENGINE and d.eng != "pe":
                    d.sig = True
        cnt = {e: 0 for e in ENGS}
        seen = {e: {} for e in ENGS}
        dcount = [0] * N_DMA_SEMS
        dnext = 0
        for o in self.ops:
            E = engs[o.eng]
            need = {}
            for d in o.deps:
                if d.ev is None:
                    continue
                key, val = d.ev
                if (not d.is_dma) and d.eng == o.eng and not o.is_dma:
                    if o.eng == "pe" or not STRICT_SAME_ENGINE:
                        continue
                if need.get(key, 0) < val:
                    need[key] = val
            if o.is_dma:
                k = dnext
                dnext = (dnext + 1) % N_DMA_SEMS
                if dcount[k] > 0:
                    key = ("d", k)
                    if need.get(key, 0) < dcount[k]:
                        need[key] = dcount[k]
            for key, val in need.items():
                if seen[o.eng].get(key, 0) >= val:
                    continue
                seen[o.eng][key] = val
                s = sem[key[1]] if key[0] == "e" else dsems[key[1]]
                E.wait_ge(s, val)
            if o.fn is None:
                continue
            ins = o.fn()
            if o.is_dma:
                dcount[k] += 16
                ins.then_inc(dsems[k], 16)
                o.ev = (("d", k), dcount[k])
            elif o.sig:
                cnt[o.eng] += 1
                ins.then_inc(sem[o.eng], 1)
                o.ev = (("e", o.eng), cnt[o.eng])


class TB:
    def __init__(self, t, name):
        self.t = t
        self.b = Buf(name)

    def __getitem__(self, k):
        return self.t[k]


def build_program(TS, debug=False):
    NT = TS // TT
    RPS = TS // 64
    TE = TS + 512
    nc = bass.Bass("TRN2", target_bir_lowering=False)
    dt_in = lambda name, shape: nc.dram_tensor(name, shape, F32, kind="ExternalInput").ap()
    xu = dt_in("xu", [5, D, TS + 2])
    xnh = dt_in("xnh", [D, 512])
    pT = dt_in("pT", [256, TS])
    w_in = dt_in("w_in", [D, IN_W])
    w_out = dt_in("w_out", [D, D])
    w_gate = dt_in("w_gate", [D, D])
    w_ple = dt_in("w_ple", [256, D])
    gvec = dt_in("gvec", [128, 3, KC])
    muv = dt_in("muv", [128, 2, 5, 26])
    w0a0 = dt_in("w0a0", [128, 2, 5, 8])
    w2u = dt_in("w2u", [5, 128, 1024])
    a2u = dt_in("a2u", [5, 128, 1024])
    kvec = dt_in("kvec", [128, 3, 8])
    lnx = dt_in("lnx", [2, 1024])
    flags = dt_in("flags", [128, 16])
    bint = dt_in("bint", [128, 16, 256])
    bedge = dt_in("bedge", [7, 128, 16, 384])
    outT = nc.dram_tensor("outT", [D, TS], F32, kind="ExternalOutput").ap()
    kindS = "ExternalOutput" if debug else "Internal"
    scr = lambda name, shape, dt: nc.dram_tensor(name, shape, dt, kind=(kindS if name in ("MIX", "YB", "QT", "KTn", "GT", "VN") else "Internal")).ap()
    wi_bf = scr("wi_bf", [D, IN_W], BF16)
    wo_bf = scr("wo_bf", [D, D], BF16)
    wg_bf = scr("wg_bf", [D, D], BF16)
    wp_bf = scr("wp_bf", [256, D], BF16)
    YB = scr("YB", [TS, 1024], F32)
    MIX = scr("MIX", [D, TS], BF16)
    QT = scr("QT", [1024, TS], BF16)
    KTn = scr("KTn", [1024, TE], BF16)
    GT = scr("GT", [1024, TS], BF16)
    VN = scr("VN", [TE, 1024], BF16)
    HSd = scr("HSd", [3, 128, 1024], F32)
    bHS = [Buf("HS%d" % i) for i in range(3)]
    bYB, bMIX, bQT, bKTn, bGT, bVN = (Buf(n) for n in ("YB", "MIX", "QT", "KTn", "GT", "VN"))
    bwi, bwo, bwg, bwp = (Buf(n) for n in ("wi", "wo", "wg", "wp"))

    S = Sched(nc)
    T, V, A, P = nc.tensor, nc.vector, nc.scalar, nc.gpsimd
    EV = {"dve": V, "pool": P}

    def mm(out, lhsT, rhs, start, stop, r, w):
        S.op("pe", lambda: T.matmul(out, lhsT, rhs, start=start, stop=stop), r, w)

    def tt(eng, out, a, b, op, r, w):
        S.op(eng, lambda: EV[eng].tensor_tensor(out, a, b, op), r, w)

    def tsc(eng, out, a, s1, s2, op0, op1, r, w):
        if s2 is None:
            S.op(eng, lambda: EV[eng].tensor_scalar(out, a, s1, None, op0), r, w)
        else:
            S.op(eng, lambda: EV[eng].tensor_scalar(out, a, s1, s2, op0, op1), r, w)

    def stt(eng, out, a, s, b, op0, op1, r, w):
        S.op(eng, lambda: EV[eng].scalar_tensor_tensor(out, a, s, b, op0, op1), r, w)

    def act(out, in_, func, r, w, bias=None, scale=1.0):
        if bias is None:
            S.op("act", lambda: A.activation(out=out, in_=in_, func=func, scale=scale), r, w)
        else:
            S.op("act", lambda: A.activation(out=out, in_=in_, func=func, bias=bias, scale=scale), r, w)

    def cp(eng, out, in_, r, w):
        if eng == "act":
            act(out, in_, AF.Copy, r, w)
        else:
            S.op(eng, lambda: EV[eng].tensor_copy(out, in_), r, w)

    def mset(eng, ap, val, w):
        S.op(eng, lambda: EV[eng].memset(ap, val), (), w)

    def dma(q, out, in_, r, w):
        e = {"sp": nc.sync, "pool": nc.gpsimd, "act": nc.scalar}[q]
        return S.op(q, lambda: e.dma_start(out=out, in_=in_), r, w, dma=True)

    out_stores = []
    with contextlib.ExitStack() as st0:
        def sb(st, name, shape, dt=F32):
            return TB(st.enter_context(nc.sbuf_tensor(name, shape, dt)), name)

        pb = [TB(st0.enter_context(nc.psum_tensor("pb%d" % i, [128, 512], F32)), "pb%d" % i) for i in range(8)]
        ident = sb(st0, "ident", [128, 128], BF16)
        identf = sb(st0, "identf", [128, 128], F32)
        jmat = sb(st0, "jmat", [64, 64], F32)
        onesb = sb(st0, "onesb", [128, 128], BF16)
        blk1 = sb(st0, "blk1", [128, 128], BF16)
        hsel = sb(st0, "hsel", [128, 2], BF16)
        gv = sb(st0, "gv", [128, 3, KC])
        mu = sb(st0, "mu", [128, 2, 5, 26])
        mu0 = sb(st0, "mu0", [128, 5, 26])
        w0 = sb(st0, "w0", [128, 2, 5, 8])
        kv = sb(st0, "kv", [128, 3, 8])
        fl = sb(st0, "fl", [128, 16])
        cmask = sb(st0, "cmask", [128, TT])
        msk_s = sb(st0, "msk_s", [64, 512])
        msk_i = sb(st0, "msk_i", [64, 512])
        msk_l = sb(st0, "msk_l", [64, 512])
        id8 = sb(st0, "id8", [64, 512])
        epsc = sb(st0, "epsc", [128, 1])
        S.op("pool", lambda: P.memset(identf[:], 0.0), (), [identf.b])
        iot_p = sb(st0, "iot_p", [128, 1])
        iot_f = sb(st0, "iot_f", [128, 128])
        S.op("pool", lambda: P.iota(iot_f[:], pattern=[[1, 128]], base=0, channel_multiplier=0, allow_small_or_imprecise_dtypes=True), (), [iot_f.b])
        S.op("pool", lambda: P.iota(iot_p[:], pattern=[[0, 1]], base=0, channel_multiplier=1, allow_small_or_imprecise_dtypes=True), (), [iot_p.b])
        tsc("dve", identf[:], iot_f[:], iot_p[:, 0:1], None, ALU.is_equal, None, [iot_f.b, iot_p.b], [identf.b])
        cp("dve", ident[:], identf[:], [identf.b], [ident.b])
        pj = sb(st0, "pj", [128, 1])
        tsc("dve", pj[:], iot_p[:], -1.0, 63.0, ALU.mult, ALU.add, [iot_p.b], [pj.b])
        tsc("dve", jmat[:], iot_f[0:64, 0:64], pj[0:64, 0:1], None, ALU.is_equal, None, [iot_f.b, pj.b], [jmat.b])
        mset("dve", onesb[:], 1.0, [onesb.b])
        mset("dve", blk1[:], 0.0, [blk1.b])
        mset("dve", blk1[0:64, 0:64], 1.0, [blk1.b])
        mset("dve", blk1[64:128, 64:128], 1.0, [blk1.b])
        mset("dve", hsel[:], 0.0, [hsel.b])
        mset("dve", hsel[0:64, 0:1], 1.0, [hsel.b])
        mset("dve", hsel[64:128, 1:2], 1.0, [hsel.b])
        mset("dve", epsc[:], 1e-6, [epsc.b])
        epsl = sb(st0, "epsl", [128, 1])
        onef = sb(st0, "onef", [128, 1])
        mset("dve", onef[:], 1.0, [onef.b])
        mset("dve", epsl[:], 64e-5, [epsl.b])
        mset("dve", cmask[:], 1.0, [cmask.b])
        mset("dve", cmask[:].rearrange("p (c k) -> p c k", k=C)[:, :, 0:1], 0.0, [cmask.b])
        tsc("dve", msk_s[:, 0:64], iot_f[0:64, 0:64], iot_p[0:64, 0:1], None, ALU.is_gt, None, [iot_f.b, iot_p.b], [msk_s.b])
        tsc("dve", msk_i[:, 0:64], iot_f[0:64, 0:64], iot_p[0:64, 0:1], None, ALU.is_ge, None, [iot_f.b, iot_p.b], [msk_i.b])
        tsc("dve", msk_l[:, 0:64], iot_f[0:64, 0:64], iot_p[0:64, 0:1], None, ALU.is_lt, None, [iot_f.b, iot_p.b], [msk_l.b])
        cp("dve", id8[:, 0:64], identf[0:64, 0:64], [identf.b], [id8.b])
        mskP_s = sb(st0, "mskP_s", [128, 256])
        mskP_l = sb(st0, "mskP_l", [128, 256])
        idP = sb(st0, "idP", [128, 256])
        pj2 = sb(st0, "pj2", [128, 1])
        tsc("dve", pj2[:], iot_p[:], -64.0, None, ALU.add, None, [iot_p.b], [pj2.b])
        for (m_, op_) in ((mskP_s, ALU.is_gt), (mskP_l, ALU.is_lt), (idP, ALU.is_equal)):
            tsc("dve", m_[0:64, 0:64], iot_f[0:64, 0:64], iot_p[0:64, 0:1], None, op_, None, [iot_f.b, iot_p.b], [m_.b])
            tsc("dve", m_[64:128, 0:64], iot_f[64:128, 0:64], pj2[64:128, 0:1], None, op_, None, [iot_f.b, pj2.b], [m_.b])
            for rr in range(1, 4):
                cp("dve", m_[:, rr * 64:(rr + 1) * 64], m_[:, 0:64], [m_.b], [m_.b])
        for m_ in (msk_s, msk_i, msk_l, id8):
            for rr in range(1, 8):
                cp("dve", m_[:, rr * 64:(rr + 1) * 64], m_[:, 0:64], [m_.b], [m_.b])
        dma("sp", gv[:], gvec, [], [gv.b])
        dma("sp", mu[:], muv, [], [mu.b])
        dma("sp", w0[:], w0a0, [], [w0.b])
        dma("sp", kv[:], kvec, [], [kv.b])
        dma("sp", fl[:], flags, [], [fl.b])
        tt("dve", mu0[:], mu[:, 0], mu[:, 1], ALU.add, [mu.b], [mu0.b])
        tsc("dve", mu0[:], mu0[:], -1.0, 1.0, ALU.mult, ALU.add, [mu0.b], [mu0.b])

        with contextlib.ExitStack() as st:
            wl = [sb(st, "wl%d" % i, [128, 2112]) for i in range(2)]
            wc = [sb(st, "wc%d" % i, [128, 2112], BF16) for i in range(2)]
            it = 0
            jobs = []
            for kc in range(KC):
                for q in range(4):
                    jobs.append((w_in[kc * 128:(kc + 1) * 128, q * 2112:(q + 1) * 2112], wi_bf[kc * 128:(kc + 1) * 128, q * 2112:(q + 1) * 2112], gv[:, 0, kc:kc + 1], 2112, bwi))
            for kc in range(KC):
                jobs.append((w_out[kc * 128:(kc + 1) * 128, :], wo_bf[kc * 128:(kc + 1) * 128, :], None, 2048, bwo))
                jobs.append((w_gate[kc * 128:(kc + 1) * 128, :], wg_bf[kc * 128:(kc + 1) * 128, :], gv[:, 1, kc:kc + 1], 2048, bwg))
            for kc in range(2):
                jobs.append((w_ple[kc * 128:(kc + 1) * 128, :], wp_bf[kc * 128:(kc + 1) * 128, :], None, 2048, bwp))
            for (src, dst, sc, n, bdst) in jobs:
                a, b_ = wl[it % 2], wc[it % 2]
                dma("sp", a[:, 0:n], src, [], [a.b])
                if sc is None:
                    if it % 2 == 0:
                        cp("act", b_[:, 0:n], a[:, 0:n], [a.b], [b_.b])
                    else:
                        cp("dve", b_[:, 0:n], a[:, 0:n], [a.b], [b_.b])
                else:
                    if it % 2 == 0:
                        act(b_[:, 0:n], a[:, 0:n], AF.Copy, [a.b, gv.b], [b_.b], scale=sc)
                    else:
                        tsc("dve", b_[:, 0:n], a[:, 0:n], sc, None, ALU.mult, None, [a.b, gv.b], [b_.b])
                dma("pool", dst, b_[:, 0:n], [b_.b], [bdst])
                it += 1
        S.barrier()

        def load_x_tile(st_bufs, src_ap, ncols):
            xs, xb, sq, rs = st_bufs
            half = KC // 2
            v = src_ap.rearrange("(kc p) n -> p kc n", p=128)
            for hf in range(2):
                dma("sp", xs[:, :, 0:ncols], v[:, hf * half:(hf + 1) * half, :], [], [xs.b])
                cp("dve", xb[:, hf * half:(hf + 1) * half, 0:ncols], xs[:, :, 0:ncols], [xs.b], [xb.b])
                act(sq[:, :, 0:ncols], xs[:, :, 0:ncols], AF.Square, [xs.b], [sq.b])
                for k8 in range(half):
                    kc = hf * half + k8
                    mm(pb[7][:, 0:ncols], onesb[:], sq[:, k8, 0:ncols], kc == 0, kc == KC - 1, [onesb.b, sq.b], [pb[7].b])
            act(rs[:, 0:ncols], pb[7][:, 0:ncols], AF.Sqrt, [pb[7].b, epsc.b], [rs.b], bias=epsc[:, 0:1], scale=1.0 / D)
            S.op("dve", lambda: V.reciprocal(rs[:, 0:ncols], rs[:, 0:ncols]), [rs.b], [rs.b])

        def load_w_group(wt, src_bf, col0, ncols, bsrc, nk=KC):
            v = src_bf.rearrange("(kc p) n -> p kc n", p=128)
            dma("sp", wt[:, 0:nk, 0:ncols], v[:, :, col0:col0 + ncols], [bsrc], [wt.b])

        DO = os.environ.get('KDO', 'rnp')
        UNITS = os.environ.get('KUNITS', '01243h')
        with contextlib.ExitStack() as st:
            xs = sb(st, "xs", [128, KC // 2, TT + 2])
            xb = sb(st, "xb", [128, KC, TT + 2], BF16)
            sq = sb(st, "sq", [128, KC // 2, TT + 2], BF16)
            rs = sb(st, "rs", [128, TT + 2])
            wt = [sb(st, "wt%d" % i, [128, KC, 256], BF16) for i in range(2)]
            zr = [sb(st, "zr%d" % i, [128, TT + 2]) for i in range(2)]
            ztmp = [sb(st, "ztmp%d" % i, [128, TT]) for i in range(2)]
            Rf = sb(st, "Rf", [128, 8, TT], BF16)
            Kf = sb(st, "Kf", [128, 8, TT])
            wdf = sb(st, "wdf", [128, TT], BF16)
            adf = sb(st, "adf", [128, TT], BF16)
            w2b = sb(st, "w2b", [128, 1024], BF16)
            a2b = sb(st, "a2b", [128, 1024], BF16)
            w2f = sb(st, "w2f", [128, 1024])
            AR = sb(st, "AR", [128, 8, NCH, 2, C], BF16)
            KTt = sb(st, "KTt", [128, 8, TT], BF16)
            BTt = sb(st, "BTt", [128, 8, TT], BF16)
            KH = sb(st, "KH", [128, 8, TT], BF16)
            BH = sb(st, "BH", [128, 8, TT], BF16)
            VB = sb(st, "VB", [128, 8, TT], BF16)
            RK = sb(st, "RK", [128, 8, TT], BF16)
            SG = sb(st, "SG", [128, 8, TT], BF16)
            PC = sb(st, "PC", [128, 8, NCH])
            tmp = [sb(st, "tmp%d" % i, [128, TT]) for i in range(8)]
            tmpb = sb(st, "tmpb", [128, TT], BF16)
            KHt = [[sb(st, "KHt%d%d" % (p_, h_), [64, 512], BF16) for h_ in range(2)] for p_ in range(2)]
            BHt = [[sb(st, "BHt%d%d" % (p_, h_), [64, 512], BF16) for h_ in range(2)] for p_ in range(2)]
            Vt = [[sb(st, "Vt%d%d" % (p_, h_), [64, 512], BF16) for h_ in range(2)] for p_ in range(2)]
            hb = lambda nm: [sb(st, "%s%d" % (nm, h_), [64, 512], BF16) for h_ in range(2)]
            hp_ = lambda nm, dt=BF16: [sb(st, "%s%d" % (nm, h_), [128, 256], dt) for h_ in range(2)]
            aMt = hp_("aMt")
            aM = hp_("aM")
            aAak = [hb("aAak0"), hb("aAak1")]
            aArb = [hb("aArb0"), hb("aArb1")]
            aArk = [hb("aArk0"), hb("aArk1")]
            Tt = [hb("Tt0"), hb("Tt1")]
            Pk = [[sb(st, "Pk%d%d" % (h_, i), [128, 256], BF16) for i in range(2)] for h_ in range(2)]
            Qk = [[sb(st, "Qk%d%d" % (h_, i), [128, 256], BF16) for i in range(2)] for h_ in range(2)]
            Rm = [sb(st, "Rm%d" % h_, [128, 256]) for h_ in range(2)]
            Rb = [[sb(st, "Rb%d%d" % (h_, i), [128, 256], BF16) for i in range(2)] for h_ in range(2)]
            Xs = hb("Xs")
            Us = hb("Us")
            Ys = sb(st, "Ys", [64, 1024])
            Hm = sb(st, "Hm", [128, 8, 128])
            Hb = sb(st, "Hb", [128, 8, 128], BF16)
            ybl = sb(st, "ybl", [64, 1024])
            gn = [sb(st, "gn%d" % i, [64, 1024]) for i in range(1)]
            gs = [sb(st, "gs%d" % i, [64, 16]) for i in range(4)]
            lnw = sb(st, "lnw", [64, 1024])
            lnb = sb(st, "lnb", [64, 1024])
            yab = sb(st, "yab", [64, 1024], BF16)
            mixt = sb(st, "mixt", [128, 8, TT], BF16)
            natq = sb(st, "natq", [128, TT], BF16)
            vtok = sb(st, "vtok", [128, 512], BF16)
            rtk = sb(st, "rtk", [128, 4])
            dma("sp", lnw[:], lnx[0:1, :].partition_broadcast(64), [], [lnw.b])
            dma("sp", lnb[:], lnx[1:2, :].partition_broadcast(64), [], [lnb.b])
            mset("dve", Hm[:], 0.0, [Hm.b])
            mset("dve", Hb[:], 0.0, [Hb.b])
            xbufs = (xs, xb, sq, rs)
            wcount = [0]

            def zchunks(cc_list, consumer, halo):
                gi = 0
                while gi < len(cc_list):
                    grp = cc_list[gi:gi + 2]
                    contiguous = all(grp[i] == grp[0] + i for i in range(len(grp)))
                    assert contiguous
                    w = wt[wcount[0] % 2]
                    wcount[0] += 1
                    load_w_group(w, wi_bf, grp[0] * 128, len(grp) * 128, bwi)
                    for li, cc in enumerate(grp):
                        pbm = pb[cc % 2]
                        z = zr[cc % 2]
                        c_lo, c_hi = (0, TT + 2) if halo else (1, TT + 1)
                        for kc in range(KC):
                            mm(pbm[:, c_lo:c_hi], w[:, kc, li * 128:(li + 1) * 128], xb[:, kc, c_lo:c_hi], kc == 0, kc == KC - 1, [w.b, xb.b], [pbm.b])
                        tt("dve", z[:, c_lo:c_hi], pbm[:, c_lo:c_hi], rs[:, c_lo:c_hi], ALU.mult, [pbm.b, rs.b], [z.b])
                        consumer(cc, z)
                    gi += 2

            def shift_into(u, cc, z, dst_ap, dstb):
                t1 = ztmp[cc % 2]
                act(t1[:], z[:, 1:TT + 1], AF.Copy, [z.b, mu0.b], [t1.b], scale=mu0[:, u, cc:cc + 1])
                stt("dve", t1[:], z[:, 0:TT], mu[:, 0, u, cc:cc + 1], t1[:], ALU.mult, ALU.add, [z.b, mu.b, t1.b], [t1.b])
                stt("dve", dst_ap, z[:, 2:TT + 2], mu[:, 1, u, cc:cc + 1], t1[:], ALU.mult, ALU.add, [z.b, mu.b, t1.b], [dstb])

            def run_unit(u, full, own):
                dma("sp", w2f[:], w2u[u], [], [w2f.b])
                cp("dve", w2b[:], w2f[:], [w2f.b], [w2b.b])
                dma("sp", w2f[:], a2u[u], [], [w2f.b])
                cp("dve", a2b[:], w2f[:], [w2f.b], [a2b.b])
                load_x_tile(xbufs, xu[u][:, 0:TT + 2], TT + 2)
                for j in range(NT):

                    def consumer(cc, z):
                        if cc < 8:
                            shift_into(u, cc, z, Rf[:, cc, :], Rf.b)
                        elif cc < 16:
                            shift_into(u, cc, z, Kf[:, cc - 8, :], Kf.b)
                        elif cc < 24:
                            shift_into(u, cc, z, VB[:, cc - 16, :], VB.b)
                        elif cc == 24:
                            shift_into(u, cc, z, tmp[0][:], tmp[0].b)
                            act(wdf[:], tmp[0][:], AF.Tanh, [tmp[0].b], [wdf.b])
                        elif cc == 25:
                            shift_into(u, cc, z, adf[:], adf.b)
                        elif cc < 34:
                            act(SG[:, cc - 26, :], z[:, 1:TT + 1], AF.Silu, [z.b], [SG.b])

                    cols = list(range(8, 26)) if not full else list(range(0, 26))
                    if not full:
                        zchunks(cols, consumer, True)
                    else:
                        zchunks(list(range(0, 26)), consumer, True)
                        if own:
                            zchunks(list(range(26, 34)), consumer, False)
                    if (not own) and j + 1 < NT:
                        load_x_tile(xbufs, xu[u][:, (j + 1) * TT:(j + 1) * TT + TT + 2], TT + 2)
                    if 'p' not in os.environ.get('KSTAGE', 'ps'):
                        continue
                    for ct in range(8):
                        sg, ic, cs, kk, t4, t5, t6, t7 = tmp
                        mm(pb[0][:, 0:TT], w2b[:, ct * 128:(ct + 1) * 128], wdf[:], True, True, [w2b.b, wdf.b], [pb[0].b])
                        act(sg[:], pb[0][:, 0:TT], AF.Sigmoid, [pb[0].b, w0.b], [sg.b], bias=w0[:, 0, u, ct:ct + 1])
                        mm(pb[1][:, 0:TT], a2b[:, ct * 128:(ct + 1) * 128], adf[:], True, True, [a2b.b, adf.b], [pb[1].b])
                        act(ic[:], pb[1][:, 0:TT], AF.Sigmoid, [pb[1].b, w0.b], [ic.b], bias=w0[:, 1, u, ct:ct + 1])
                        S.op("dve", lambda: V.tensor_tensor_scan(cs[:], cmask[:], sg[:], 0.0, ALU.mult, ALU.add), [cmask.b, sg.b], [cs.b])
                        tsc("dve", kk[:], Kf[:, ct, :], kv[:, 0, ct:ct + 1], None, ALU.mult, None, [Kf.b, kv.b], [kk.b])
                        tt("dve", tmpb[:], kk[:], kk[:], ALU.mult, [kk.b], [tmpb.b])
                        mm(pb[2][:, 0:TT], blk1[:], tmpb[:], True, True, [blk1.b, tmpb.b], [pb[2].b])
                        act(t4[:], pb[2][:, 0:TT], AF.Sqrt, [pb[2].b], [t4.b])
                        tsc("dve", t4[:], t4[:], 1e-12, None, ALU.max, None, [t4.b], [t4.b])
                        S.op("dve", lambda: V.reciprocal(t4[:], t4[:]), [t4.b], [t4.b])
                        tt("dve", kk[:], kk[:], t4[:], ALU.mult, [kk.b, t4.b], [kk.b])
                        tt("dve", t5[:], kk[:], ic[:], ALU.mult, [kk.b, ic.b], [t5.b])
                        tsc("dve", t6[:], ic[:], -1.0, kv[:, 1, ct:ct + 1], ALU.add, ALU.mult, [ic.b, kv.b], [t6.b])
                        stt("dve", t6[:], t6[:], 1.0, Kf[:, ct, :], ALU.add, ALU.mult, [t6.b, Kf.b], [t6.b])
                        act(t4[:], cs[:], AF.Exp, [cs.b], [t4.b], scale=-DSC)
                        cp("dve", PC[:, ct, :], t4[:].rearrange("p (c k) -> p c k", k=C)[:, :, C - 1], [t4.b], [PC.b])
                        arv = AR[:, ct].rearrange("p c two k -> p two c k")
                        tt("dve", arv[:, 1], Rf[:, ct, :].rearrange("p (c k) -> p c k", k=C), t4[:].rearrange("p (c k) -> p c k", k=C), ALU.mult, [Rf.b, t4.b], [AR.b])
                        tt("dve", t7[:], cs[:], sg[:], ALU.subtract, [cs.b, sg.b], [t7.b])
                        act(t7[:], t7[:], AF.Exp, [t7.b], [t7.b], scale=-DSC)
                        stt("dve", arv[:, 0], kk[:].rearrange("p (c k) -> p c k", k=C), -1.0, t7[:].rearrange("p (c k) -> p c k", k=C), ALU.mult, ALU.mult, [kk.b, t7.b], [AR.b])
                        act(t4[:], cs[:], AF.Exp, [cs.b], [t4.b], scale=DSC)
                        tt("dve", KTt[:, ct, :], t6[:], t4[:], ALU.mult, [t6.b, t4.b], [KTt.b])
                        tt("dve", BTt[:, ct, :], t5[:], t4[:], ALU.mult, [t5.b, t4.b], [BTt.b])
                        tt("dve", t4[:].rearrange("p (c k) -> p c k", k=C), t4[:].rearrange("p (c k) -> p c k", k=C),
                           PC[:, ct, :].unsqueeze(2).to_broadcast([128, NCH, C]), ALU.mult, [t4.b, PC.b], [t4.b])
                        tt("dve", KH[:, ct, :], t6[:], t4[:], ALU.mult, [t6.b, t4.b], [KH.b])
                        tt("dve", BH[:, ct, :], t5[:], t4[:], ALU.mult, [t5.b, t4.b], [BH.b])
                        if own:
                            stt("dve", RK[:, ct, :], Rf[:, ct, :], kv[:, 2, ct:ct + 1], Kf[:, ct, :], ALU.mult, ALU.mult, [Rf.b, kv.b, Kf.b], [RK.b])
                    if 's' not in os.environ.get('KSTAGE', 'ps'):
                        continue
                    def off_tok(n, p):
                        csl = slice(n * C, (n + 1) * C)
                        for qi, (src, dstT) in enumerate(((KH, KHt), (BH, BHt), (VB, Vt))):
                            for half in range(2):
                                pbt = pb[(qi * 2 + half) % 4]
                                for q in range(4):
                                    ct = half * 4 + q
                                    mm(pbt[0:64, q * 128:(q + 1) * 128], src[:, ct, csl], ident[:], True, True, [src.b, ident.b], [pbt.b])
                                d_ = dstT[p][half]
                                cp("act" if half == 0 else "dve", d_[:], pbt[0:64, :], [pbt.b], [d_.b])

                    def mm2(bank, c0, q, lhs, rhs, r, w):
                        sl = slice(c0 + q * 64, c0 + (q + 1) * 64)
                        mm(bank[0:64, sl], lhs(slice(0, 64)), rhs(slice(0, 64)), True, True, r, w)
                        l_, r_ = lhs(slice(64, 128)), rhs(slice(64, 128))
                        S.op("pe", lambda: T.matmul(bank[64:128, sl], l_, r_, start=True, stop=True, tile_position=(64, 64)), r, w)

                    def off_A(n, p):
                        csl = slice(n * C, (n + 1) * C)
                        for half in range(2):
                            bank = pb[half]
                            for q in range(4):
                                ct = half * 4 + q
                                mm2(bank, 0, q, lambda ps_: BTt[ps_, ct, csl], lambda ps_: AR[ps_, ct, n, 0, :], [BTt.b, AR.b], [bank.b])
                                mm2(bank, 256, q, lambda ps_: AR[ps_, ct, n, 0, :], lambda ps_: BTt[ps_, ct, csl], [BTt.b, AR.b], [bank.b])
                            tt("dve", aMt[half][:], bank[:, 0:256], mskP_s[:], ALU.mult, [bank.b, mskP_s.b], [aMt[half].b])
                            tt("dve", aM[half][:], bank[:, 256:512], mskP_l[:], ALU.mult, [bank.b, mskP_l.b], [aM[half].b])
                            tt("dve", Rm[half][:], aMt[half][:], idP[:], ALU.add, [aMt[half].b, idP.b], [Rm[half].b])
                            cp("act", Rb[half][0][:], Rm[half][:], [Rm[half].b], [Rb[half][0].b])
                            specs = ((KTt, 0, aAak[p][half], msk_s, 0), (BTt, 1, aArb[p][half], msk_i, 1), (KTt, 1, aArk[p][half], msk_i, 0))
                            for (lsrc, ar_i, dst, msk, slot) in specs:
                                if (not full) and ar_i == 1:
                                    continue
                                for e in range(2):
                                    pbx = pb[2 + e]
                                    ps_ = slice(e * 64, e * 64 + 64)
                                    for qq in range(4):
                                        ct = half * 4 + qq
                                        mm(pbx[0:64, slot * 256 + qq * 64:slot * 256 + (qq + 1) * 64], lsrc[ps_, ct, csl], AR[ps_, ct, n, ar_i, :], True, True, [lsrc.b, AR.b], [pbx.b])
                                    tt("dve", dst[:].rearrange("p (q e k) -> p q e k", e=2, k=64)[:, :, e, :], pbx[0:64, slot * 256:(slot + 1) * 256].rearrange("p (q k) -> p q k", k=64),
                                       msk[:, 0:256].rearrange("p (q k) -> p q k", k=64), ALU.mult, [pbx.b, msk.b], [dst.b])

                    def off_dR(n, p, lev):
                        for half in range(2):
                            Pn = Pk[half][lev % 2]
                            Rcur = Rb[half][(lev - 1) % 2]
                            Rnext = Rb[half][lev % 2]
                            bank = pb[2 + half]
                            for q in range(4):
                                sl = slice(q * 64, (q + 1) * 64)
                                mm2(bank, 0, q, lambda ps_: Pn[ps_, sl], lambda ps_: Rcur[ps_, sl], [Pn.b, Rcur.b], [bank.b])
                            tt("dve", Rm[half][:], Rm[half][:], bank[:, 0:256], ALU.add, [Rm[half].b, bank.b], [Rm[half].b])
                            if lev < 5:
                                cp("act", Rnext[:], Rm[half][:], [Rm[half].b], [Rnext.b])
                            else:
                                tdst = Tt[p][half]
                                tv = tdst[:].rearrange("p (q e k) -> p q e k", e=2, k=64)
                                cp("act", tv[:, :, 0, :], Rm[half][0:64, :].rearrange("p (q k) -> p q k", k=64), [Rm[half].b], [tdst.b])
                                cp("act", Rnext[64:128, :], Rm[half][64:128, :], [Rm[half].b], [Rnext.b])
                                for q in range(4):
                                    sl = slice(q * 64, (q + 1) * 64)
                                    mm(pb[3][0:64, 256 + q * 64:256 + (q + 1) * 64], ident[64:128, 64:128], Rnext[64:128, sl], True, True, [ident.b, Rnext.b], [pb[3].b])
                                cp("dve", tv[:, :, 1, :], pb[3][0:64, 256:512].rearrange("p (q k) -> p q k", k=64), [pb[3].b], [tdst.b])

                    def off_lev(n, p, lev, delayed=True):
                        if delayed and lev > 1:
                            off_dR(n, p, lev - 1)
                        for half in range(2):
                            Pc, Qc = (aM[half], aMt[half]) if lev == 1 else (Pk[half][(lev - 1) % 2], Qk[half][(lev - 1) % 2])
                            Pn, Qn = Pk[half][lev % 2], Qk[half][lev % 2]
                            bank = pb[half]
                            for q in range(4):
                                sl = slice(q * 64, (q + 1) * 64)
                                mm2(bank, 0, q, lambda ps_: Qc[ps_, sl], lambda ps_: Pc[ps_, sl], [Qc.b, Pc.b], [bank.b])
                            cp("act", Pn[:], bank[:, 0:256], [bank.b], [Pn.b])
                            if lev < 5:
                                for q in range(4):
                                    sl = slice(q * 64, (q + 1) * 64)
                                    mm2(bank, 256, q, lambda ps_: Pc[ps_, sl], lambda ps_: Qc[ps_, sl], [Qc.b, Pc.b], [bank.b])
                                cp("dve", Qn[:], bank[:, 256:512], [bank.b], [Qn.b])
                        if not delayed:
                            off_dR(n, p, lev)

                    def on_X(n, p):
                        for half in range(2):
                            pbx = pb[4 + half]
                            for q in range(4):
                                ct = half * 4 + q
                                mm(pbx[0:64, q * 128:(q + 1) * 128], AR[:, ct, n, 0, :], Hb[:, ct, :], True, False, [AR.b, Hb.b], [pbx.b])
                                for e in range(2):
                                    hh = q * 2 + e
                                    mm(pbx[0:64, q * 128 + e * 64:q * 128 + (e + 1) * 64], aAak[p][half][:, hh * 64:(hh + 1) * 64], Vt[p][half][:, hh * 64:(hh + 1) * 64], False, e == 1,
                                       [aAak[p][half].b, Vt[p][half].b], [pbx.b])
                            cp("act" if half == 0 else "dve", Xs[half][:], pbx[0:64, :], [pbx.b], [Xs[half].b])

                    def on_U(n, p):
                        for half in range(2):
                            pbx = pb[6 + half]
                            for q in range(8):
                                sl = slice(q * 64, (q + 1) * 64)
                                mm(pbx[0:64, sl], Tt[p][half][:, sl], Xs[half][:, sl], True, True, [Tt[p][half].b, Xs[half].b], [pbx.b])
                            cp("act" if half == 0 else "dve", Us[half][:], pbx[0:64, :], [pbx.b], [Us[half].b])

                    def on_Y(n, p):
                        if not full:
                            return
                        for half in range(2):
                            pbx = pb[4 + half]
                            for q in range(4):
                                ct = half * 4 + q
                                mm(pbx[0:64, q * 128:(q + 1) * 128], AR[:, ct, n, 1, :], Hb[:, ct, :], True, False, [AR.b, Hb.b], [pbx.b])
                                for e in range(2):
                                    hh = q * 2 + e
                                    sl = slice(hh * 64, (hh + 1) * 64)
                                    o_ = pbx[0:64, q * 128 + e * 64:q * 128 + (e + 1) * 64]
                                    mm(o_, aArb[p][half][:, sl], Us[half][:, sl], False, False, [aArb[p][half].b, Us[half].b], [pbx.b])
                                    mm(o_, aArk[p][half][:, sl], Vt[p][half][:, sl], False, e == 1, [aArk[p][half].b, Vt[p][half].b], [pbx.b])
                            cp("act" if half == 0 else "dve", Ys[:, half * 512:(half + 1) * 512], pbx[0:64, :], [pbx.b], [Ys.b])

                    def on_H(n, p):
                        for half in range(2):
                            pbx = pb[6 + half]
                            for q in range(4):
                                sl = slice(q * 128, (q + 1) * 128)
                                mm(pbx[:, sl], BHt[p][half][:, sl], Us[half][:, sl], True, False, [BHt[p][half].b, Us[half].b], [pbx.b])
                                mm(pbx[:, sl], KHt[p][half][:, sl], Vt[p][half][:, sl], False, True, [KHt[p][half].b, Vt[p][half].b], [pbx.b])
                            for e in range(2):
                                pp = slice(e * 64, e * 64 + 64)
                                hv = Hm[pp, half * 4:(half + 1) * 4, e * 64:(e + 1) * 64]
                                pcv = PC[pp, half * 4:(half + 1) * 4, n:n + 1].to_broadcast([64, 4, 64])
                                tt("dve", hv, hv, pcv, ALU.mult, [Hm.b, PC.b], [Hm.b])
                                pv = pbx[pp, :].rearrange("p (q c) -> p q c", c=128)[:, :, e * 64:(e + 1) * 64]
                                tt("dve", hv, hv, pv, ALU.add, [Hm.b, pbx.b], [Hm.b])
                                cp("act", Hb[pp, half * 4:(half + 1) * 4, e * 64:(e + 1) * 64], hv, [Hm.b], [Hb.b])

                    off_tok(0, 0)
                    off_A(0, 0)
                    for lev in range(1, 6):
                        off_lev(0, 0, lev, delayed=False)
                    for n in range(NCH):
                        csl = slice(n * C, (n + 1) * C)
                        p = n % 2
                        nx = n + 1 < NCH
                        if nx:
                            off_tok(n + 1, 1 - p)
                            off_A(n + 1, 1 - p)
                            off_lev(n + 1, 1 - p, 1)
                        on_X(n, p)
                        if nx:
                            off_lev(n + 1, 1 - p, 2)
                        on_U(n, p)
                        if nx:
                            off_lev(n + 1, 1 - p, 3)
                        on_Y(n, p)
                        if nx:
                            off_lev(n + 1, 1 - p, 4)
                        on_H(n, p)
                        if nx:
                            off_lev(n + 1, 1 - p, 5)
                        Vtf = Vt[p]
                        DR5 = nx
                        if full and not own:
                            gchunk = j * NCH + n
                            nat0 = TS - (gchunk + 1) * C
                            for half in range(2):
                                pbx = pb[4 + half]
                                mm(pbx[0:64, :], jmat[:], Ys[:, half * 512:(half + 1) * 512], True, True, [jmat.b, Ys.b], [pbx.b])
                                cp("act" if half == 0 else "dve", ybl[:, half * 512:(half + 1) * 512], pbx[0:64, :], [pbx.b], [ybl.b])
                            dma("pool", YB[nat0:nat0 + C, :], ybl[:], [ybl.b], [bYB])
                        if full and own:
                            t0 = j * TT + n * C
                            dma("sp", ybl[:], YB[t0:t0 + C, :], [bYB], [ybl.b])
                            y2 = gn[0]
                            tt("dve", ybl[:], Ys[:], ybl[:], ALU.add, [Ys.b, ybl.b], [ybl.b])
                            y3 = ybl[:].rearrange("p (h k) -> p h k", k=64)
                            S.op("dve", lambda: V.tensor_reduce(gs[0][:], y3, AX.X, ALU.add), [ybl.b], [gs[0].b])
                            tsc("dve", gs[0][:], gs[0][:], 1.0 / 64, None, ALU.mult, None, [gs[0].b], [gs[0].b])
                            tt("dve", y3, y3, gs[0][:].unsqueeze(2).to_broadcast([64, 16, 64]), ALU.subtract, [ybl.b, gs[0].b], [ybl.b])
                            act(y2[:], ybl[:], AF.Square, [ybl.b], [y2.b])
                            y23 = y2[:].rearrange("p (h k) -> p h k", k=64)
                            S.op("dve", lambda: V.tensor_reduce(gs[1][:], y23, AX.X, ALU.add), [y2.b], [gs[1].b])
                            act(gs[1][:], gs[1][:], AF.Sqrt, [gs[1].b, epsl.b], [gs[1].b], bias=epsl[0:64, 0:1], scale=1.0 / 64)
                            S.op("dve", lambda: V.reciprocal(gs[1][:], gs[1][:]), [gs[1].b], [gs[1].b])
                            tt("dve", y3, y3, gs[1][:].unsqueeze(2).to_broadcast([64, 16, 64]), ALU.mult, [ybl.b, gs[1].b], [ybl.b])
                            tt("dve", ybl[:], ybl[:], lnw[:], ALU.mult, [ybl.b, lnw.b], [ybl.b])
                            tt("dve", ybl[:], ybl[:], lnb[:], ALU.add, [ybl.b, lnb.b], [ybl.b])
                            for ct in range(8):
                                mm(pb[4][0:64, ct * 2:(ct + 1) * 2], RK[:, ct, csl], hsel[:], True, True, [RK.b, hsel.b], [pb[4].b])
                            cp("dve", gs[2][:], pb[4][0:64, 0:16], [pb[4].b], [gs[2].b])
                            for half in range(2):
                                tt("dve", y23[:, half * 8:(half + 1) * 8, :], Vtf[half][:].rearrange("p (h k) -> p h k", k=64), gs[2][:, half * 8:(half + 1) * 8].unsqueeze(2).to_broadcast([64, 8, 64]),
                                   ALU.mult, [Vtf[half].b, gs[2].b], [y2.b])
                            tt("dve", yab[:], ybl[:], y2[:], ALU.add, [ybl.b, y2.b], [yab.b])
                            for ct in range(8):
                                mm(pb[5][:, ct * 64:(ct + 1) * 64], yab[:, ct * 128:(ct + 1) * 128], ident[0:64, 0:64], True, True, [yab.b, ident.b], [pb[5].b])
                            tt("dve", mixt[:, :, csl], pb[5][:, :].rearrange("p (c k) -> p c k", k=64), SG[:, :, csl], ALU.mult, [pb[5].b, SG.b], [mixt.b])
                        if DR5:
                            off_dR(n + 1, 1 - p, 5)
                    if full and own:
                        dma("pool", MIX[0:1024, j * TT:(j + 1) * TT].rearrange("(c p) n -> p c n", p=128), mixt[:], [mixt.b], [bMIX])
                        def nat_consumer(cc, z):
                            if cc < 42:
                                cp("act", natq[:], z[:, 1:TT + 1], [z.b], [natq.b])
                                dma("pool", QT[(cc - 34) * 128:(cc - 33) * 128, j * TT:(j + 1) * TT], natq[:], [natq.b], [bQT])
                            elif cc < 50:
                                cp("act", natq[:], z[:, 1:TT + 1], [z.b], [natq.b])
                                dma("pool", KTn[(cc - 42) * 128:(cc - 41) * 128, 256 + j * TT:256 + (j + 1) * TT], natq[:], [natq.b], [bKTn])
                            else:
                                act(natq[:], z[:, 1:TT + 1], AF.Silu, [z.b], [natq.b])
                                dma("pool", GT[(cc - 58) * 128:(cc - 57) * 128, j * TT:(j + 1) * TT], natq[:], [natq.b], [bGT])
                        zchunks(list(range(34, 50)), nat_consumer, False)
                        zchunks(list(range(58, 66)), nat_consumer, False)
                        vtok_tile(xb, rs, 1, TT, 256 + j * TT)
                        if j + 1 < NT:
                            load_x_tile(xbufs, xu[u][:, (j + 1) * TT:(j + 1) * TT + TT + 2], TT + 2)

            def vtok_tile(xb_, rs_, c0, ntok, ext0):
                nb = (ntok + 127) // 128
                for tb in range(nb):
                    bs = min(128, ntok - tb * 128)
                    mm(pb[2][0:bs, tb:tb + 1], rs_[0:1, c0 + tb * 128:c0 + tb * 128 + bs], onef[0:1, 0:1], True, True, [rs_.b, onef.b], [pb[2].b])
                cp("dve", rtk[:, 0:nb], pb[2][:, 0:nb], [pb[2].b], [rtk.b])
                for cg in range(4):
                    w = wt[wcount[0] % 2]
                    wcount[0] += 1
                    load_w_group(w, wi_bf, (50 + cg * 2) * 128, 256, bwi)
                    for tb in range(nb):
                        bs = min(128, ntok - tb * 128)
                        pbx = pb[tb % 2]
                        for kc in range(KC):
                            mm(pbx[0:bs, 0:256], xb_[:, kc, c0 + tb * 128:c0 + tb * 128 + bs], w[:, kc, :], kc == 0, kc == KC - 1, [xb_.b, w.b], [pbx.b])
                        act(vtok[0:bs, 0:256], pbx[0:bs, 0:256], AF.Copy, [pbx.b, rtk.b], [vtok.b], scale=rtk[0:bs, tb:tb + 1])
                        dma("pool", VN[ext0 + tb * 128:ext0 + tb * 128 + bs, cg * 256:(cg + 1) * 256], vtok[0:bs, 0:256], [vtok.b], [bVN])

            for u in range(3):
                if str(u) not in UNITS or 'r' not in DO:
                    continue
                tsc("dve", Hm[:], Hm[:], fl[:, u:u + 1], None, ALU.mult, None, [Hm.b, fl.b], [Hm.b])
                cp("act", Hb[:], Hm[:], [Hm.b], [Hb.b])
                run_unit(u, False, False)
                dma("pool", HSd[u], Hm[:].rearrange("p a b -> p (a b)"), [Hm.b], [bHS[u]])

            def init_state(c0):
                hm2 = Hm[:].rearrange("p a b -> p (a b)")
                mset("dve", Hm[:], 0.0, [Hm.b])
                for u in range(3):
                    dma("sp", w2f[:], HSd[u], [bHS[u]], [w2f.b])
                    stt("dve", hm2, w2f[:], fl[:, c0 + u:c0 + u + 1], hm2, ALU.mult, ALU.add, [w2f.b, fl.b, Hm.b], [Hm.b])
                cp("act", Hb[:], Hm[:], [Hm.b], [Hb.b])

            if 'r' in DO and '4' in UNITS:
                init_state(8)
                run_unit(4, True, False)
            if 'r' in DO and '3' in UNITS:
                init_state(5)
                run_unit(3, True, True)
            for hb_ in range(2 if ('r' in DO and 'h' in UNITS) else 0):
                load_x_tile(xbufs, xnh[:, hb_ * 256:(hb_ + 1) * 256], 256)
                ext0 = 0 if hb_ == 0 else 256 + TS
                for g in range(4):
                    w = wt[wcount[0] % 2]
                    wcount[0] += 1
                    load_w_group(w, wi_bf, (42 + g * 2) * 128, 256, bwi)
                    for li in range(2):
                        cc = 42 + g * 2 + li
                        pbm = pb[cc % 2]
                        for kc in range(KC):
                            mm(pbm[:, 0:256], w[:, kc, li * 128:(li + 1) * 128], xb[:, kc, 0:256], kc == 0, kc == KC - 1, [w.b, xb.b], [pbm.b])
                        tt("dve", natq[:, 0:256], pbm[:, 0:256], rs[:, 0:256], ALU.mult, [pbm.b, rs.b], [natq.b])
                        dma("pool", KTn[(cc - 42) * 128:(cc - 41) * 128, ext0:ext0 + 256], natq[:, 0:256], [natq.b], [bKTn])
                vtok_tile(xb, rs, 0, 256, ext0)
        S.barrier()

        with contextlib.ExitStack() as st:
            EBi = sb(st, "EBi", [128, 16, 256], BF16)
            EBe = sb(st, "EBe", [128, 16, 384], BF16)
            ebl = sb(st, "ebl", [128, 16, 384])
            KTs = sb(st, "KTs", [128, 8, 960], BF16)
            QTs = sb(st, "QTs", [128, 8, 512], BF16)
            GTs = sb(st, "GTs", [128, 8, 512], BF16)
            Ve = sb(st, "Ve", [128, 8, 1024], BF16)
            Vo = sb(st, "Vo", [128, 7, 1024], BF16)
            pe_ = [sb(st, "pe%d" % i, [128, 768], BF16) for i in range(2)]
            pp_ = [sb(st, "pp%d" % i, [128, 768], BF16) for i in range(2)]
            rec = sb(st, "rec", [64, 16])
            osb = sb(st, "osb", [64, 1024], BF16)
            mixb = sb(st, "mixb", [128, 8, 512], BF16)
            dma("sp", ebl[:, :, 0:256], bint, [], [ebl.b])
            act(EBi[:], ebl[:, :, 0:256], AF.Exp, [ebl.b], [EBi.b])
            for rb in range(RPS // 8 if 'n' in DO else 0):
                e0 = rb * 512
                dma("sp", KTs[:], KTn[:, e0:e0 + 960].rearrange("(c p) n -> p c n", p=128), [bKTn], [KTs.b])
                dma("sp", QTs[:], QT[:, rb * 512:(rb + 1) * 512].rearrange("(c p) n -> p c n", p=128), [bQT], [QTs.b])
                dma("sp", GTs[:], GT[:, rb * 512:(rb + 1) * 512].rearrange("(c p) n -> p c n", p=128), [bGT], [GTs.b])
                dma("sp", Ve[:], VN[e0:e0 + 1024, :].rearrange("(r p) n -> p r n", p=128), [bVN], [Ve.b])
                dma("sp", Vo[:], VN[e0 + 64:e0 + 64 + 896, :].rearrange("(r p) n -> p r n", p=128), [bVN], [Vo.b])
                for rl in range(8):
                    r = rb * 8 + rl
                    if r < 4:
                        ks, nck, edge = 0, 6, r
                    elif r >= RPS - 3:
                        ks, nck, edge = RPS - 5, 6, 4 + (r - (RPS - 3))
                    else:
                        ks, nck, edge = r, 4, None
                    if edge is not None:
                        dma("sp", ebl[:], bedge[edge], [], [ebl.b])
                        act(EBe[:], ebl[:], AF.Exp, [ebl.b], [EBe.b])
                        EB = EBe
                    else:
                        EB = EBi
                    kl = ks - rb * 8
                    W_ = nck * 64
                    for hp in range(8):
                        pe, pp = pe_[hp % 2], pp_[hp % 2]
                        for e in range(2):
                            ps_ = slice(e * 64, e * 64 + 64)
                            bb = pb[(hp % 2) * 2 + e]
                            for ck in range(nck):
                                mm(bb[:, ck * 64:(ck + 1) * 64], KTs[ps_, hp, (kl + 2 * ck) * 64:(kl + 2 * ck) * 64 + 128], QTs[ps_, hp, rl * 64:(rl + 1) * 64], True, True, [KTs.b, QTs.b], [bb.b])
                            act(pe[:, e * W_:(e + 1) * W_], bb[:, 0:W_], AF.Exp, [bb.b], [pe.b], scale=0.125)
                        tt("dve", pp[:, 0:2 * W_].rearrange("p (e w) -> p e w", e=2), pe[:, 0:2 * W_].rearrange("p (e w) -> p e w", e=2),
                           EB[:, hp * 2:hp * 2 + 2, 0:W_], ALU.mult, [pe.b, EB.b], [pp.b])
                        for e in range(2):
                            h = hp * 2 + e
                            for ck in range(nck):
                                row = kl + 2 * ck
                                if row % 2 == 0:
                                    vv = Ve[:, row // 2, h * 64:(h + 1) * 64]
                                    vb_ = Ve.b
                                else:
                                    vv = Vo[:, (row - 1) // 2, h * 64:(h + 1) * 64]
                                    vb_ = Vo.b
                                lt = pp[:, e * W_ + ck * 64:e * W_ + (ck + 1) * 64]
                                mm(pb[4 + h // 8][0:64, (h % 8) * 64:(h % 8 + 1) * 64], lt, vv, ck == 0, ck == nck - 1, [pp.b, vb_], [pb[4 + h // 8].b])
                                mm(pb[6][0:64, h:h + 1], lt, onesb[:, 0:1], ck == 0, ck == nck - 1, [pp.b, onesb.b], [pb[6].b])
                    S.op("dve", lambda: V.reciprocal(rec[:], pb[6][0:64, 0:16]), [pb[6].b], [rec.b])
                    for half in range(2):
                        tt("dve", osb[:, half * 512:(half + 1) * 512].rearrange("p (h k) -> p h k", k=64), pb[4 + half][0:64, :].rearrange("p (h k) -> p h k", k=64),
                           rec[:, half * 8:(half + 1) * 8].unsqueeze(2).to_broadcast([64, 8, 64]), ALU.mult, [pb[4 + half].b, rec.b], [osb.b])
                    for ct in range(8):
                        mm(pb[7][:, ct * 64:(ct + 1) * 64], osb[:, ct * 128:(ct + 1) * 128], ident[0:64, 0:64], True, True, [osb.b, ident.b], [pb[7].b])
                    tt("dve", mixb[:, :, rl * 64:(rl + 1) * 64], pb[7][:, :].rearrange("p (c k) -> p c k", k=64), GTs[:, :, rl * 64:(rl + 1) * 64], ALU.mult, [pb[7].b, GTs.b], [mixb.b])
                dma("pool", MIX[1024:2048, rb * 512:(rb + 1) * 512].rearrange("(c p) n -> p c n", p=128), mixb[:], [mixb.b], [bMIX])
        S.barrier()

        with contextlib.ExitStack() as st:
            mt = sb(st, "mt", [128, KC, TP], BF16)
            xr = sb(st, "xr", [128, KC, TP])
            Hh = sb(st, "Hh", [128, KC, TP])
            hb2 = sb(st, "hb2", [128, KC, TP], BF16)
            sq2 = sb(st, "sq2", [128, KC, TP], BF16)
            rs1 = sb(st, "rs1", [128, TP])
            rs2 = sb(st, "rs2", [128, TP])
            wA = [sb(st, "wA%d" % i, [128, KC, 512], BF16) for i in range(2)]
            wP = sb(st, "wP", [128, 2, D], BF16)
            ptf = sb(st, "ptf", [128, 2, TP])
            ptb = sb(st, "ptb", [128, 2, TP], BF16)
            gt_ = [sb(st, "gt%d" % i, [128, TP]) for i in range(2)]
            ot = [sb(st, "ot%d" % i, [128, TP]) for i in range(2)]
            dma("sp", wP[:], wp_bf.rearrange("(kc p) n -> p kc n", p=128), [bwp], [wP.b])
            wc2 = 0
            for j in range(TS // TP if 'p' in DO else 0):
                tsl = slice(j * TP, (j + 1) * TP)
                dma("sp", mt[:], MIX[:, tsl].rearrange("(kc p) n -> p kc n", p=128), [bMIX], [mt.b])
                dma("sp", xr[:], xu[3][:, 1 + j * TP:1 + (j + 1) * TP].rearrange("(kc p) n -> p kc n", p=128), [], [xr.b])
                dma("sp", ptf[:], pT[:, tsl].rearrange("(kc p) n -> p kc n", p=128), [], [ptf.b])
                cp("act", ptb[:], ptf[:], [ptf.b], [ptb.b])
                for g in range(4):
                    w = wA[wc2 % 2]
                    wc2 += 1
                    dma("sp", w[:], wo_bf.rearrange("(kc p) n -> p kc n", p=128)[:, :, g * 512:(g + 1) * 512], [bwo], [w.b])
                    for li in range(4):
                        dc = g * 4 + li
                        pbx = pb[dc % 2]
                        for kc in range(KC):
                            mm(pbx[:, :], w[:, kc, li * 128:(li + 1) * 128], mt[:, kc, :], kc == 0, kc == KC - 1, [w.b, mt.b], [pbx.b])
                        tt("dve", Hh[:, dc, :], pbx[:, :], xr[:, dc, :], ALU.add, [pbx.b, xr.b], [Hh.b])
                        act(sq2[:, dc, :], Hh[:, dc, :], AF.Square, [Hh.b], [sq2.b])
                        cp("dve", hb2[:, dc, :], Hh[:, dc, :], [Hh.b], [hb2.b])
                for kc in range(KC):
                    mm(pb[7][:, :], onesb[:], sq2[:, kc, :], kc == 0, kc == KC - 1, [onesb.b, sq2.b], [pb[7].b])
                act(rs1[:], pb[7][:, :], AF.Sqrt, [pb[7].b, epsc.b], [rs1.b], bias=epsc[:, 0:1], scale=1.0 / D)
                S.op("dve", lambda: V.reciprocal(rs1[:], rs1[:]), [rs1.b], [rs1.b])
                for g in range(4):
                    w = wA[wc2 % 2]
                    wc2 += 1
                    dma("sp", w[:], wg_bf.rearrange("(kc p) n -> p kc n", p=128)[:, :, g * 512:(g + 1) * 512], [bwg], [w.b])
                    for li in range(4):
                        dc = g * 4 + li
                        pbx = pb[dc % 2]
                        pby = pb[2 + dc % 2]
                        gt = gt_[dc % 2]
                        for kc in range(KC):
                            mm(pbx[:, :], w[:, kc, li * 128:(li + 1) * 128], hb2[:, kc, :], kc == 0, kc == KC - 1, [w.b, hb2.b], [pbx.b])
                        tt("dve", gt[:], pbx[:, :], rs1[:], ALU.mult, [pbx.b, rs1.b], [gt.b])
                        act(gt[:], gt[:], AF.Sigmoid, [gt.b], [gt.b])
                        for kc in range(2):
                            mm(pby[:, :], wP[:, kc, dc * 128:(dc + 1) * 128], ptb[:, kc, :], kc == 0, kc == 1, [wP.b, ptb.b], [pby.b])
                        tt("dve", gt[:], gt[:], pby[:, :], ALU.mult, [gt.b, pby.b], [gt.b])
                        tt("dve", Hh[:, dc, :], Hh[:, dc, :], gt[:], ALU.add, [Hh.b, gt.b], [Hh.b])
                        act(sq2[:, dc, :], Hh[:, dc, :], AF.Square, [Hh.b], [sq2.b])
                for kc in range(KC):
                    mm(pb[6][:, :], onesb[:], sq2[:, kc, :], kc == 0, kc == KC - 1, [onesb.b, sq2.b], [pb[6].b])
                act(rs2[:], pb[6][:, :], AF.Sqrt, [pb[6].b, epsc.b], [rs2.b], bias=epsc[:, 0:1], scale=1.0 / D)
                S.op("dve", lambda: V.reciprocal(rs2[:], rs2[:]), [rs2.b], [rs2.b])
                for dc in range(KC):
                    o = ot[dc % 2]
                    stt("dve", o[:], Hh[:, dc, :], gv[:, 2, dc:dc + 1], rs2[:], ALU.mult, ALU.mult, [Hh.b, gv.b, rs2.b], [o.b])
                    out_stores.append(dma("pool", outT[dc * 128:(dc + 1) * 128, tsl], o[:], [o.b], []))
        S.finish(out_stores)
        S.emit(st0)
    return nc


def _nat_table(rpb, R, lr0, nck, ROWS):
    kp = np.arange(128)
    ck = np.arange(nck)
    kr = lr0 + 2 * ck[None, :] + (kp[:, None] // 64)
    kc = (kp % 64)[:, None, None]
    qc = np.arange(64)[None, None, :]
    rs_ = min(max(R - 4, 0), ROWS - 8)
    cs = np.clip(qc - 8, 0, 64 - 16)
    rowok = (kr >= rs_) & (kr < rs_ + 8)
    colok = (kc >= cs) & (kc < cs + 16)
    ok = rowok[:, :, None] & colok
    dr = np.clip(kr - R + 7, 0, 14)[:, :, None]
    dc = np.clip(kc - qc + 15, 0, 30)
    dr_b = np.broadcast_to(dr, ok.shape)
    dc_b = np.broadcast_to(dc, ok.shape)
    tab = np.empty((128, 16, nck, 64), np.float32)
    for h in range(16):
        tab[:, h] = np.where(ok, rpb[h][dr_b, dc_b], np.float32(NEG))
    return tab.reshape(128, 16, nck * 64)


_CACHE = {}


def kernel(x, p, norm_mix_g, w_in, shift_mu_prev, shift_mu_next, decay_w0, decay_w2, iclr_a0, iclr_a2,
           k_k, k_a, r_k, lnx_w, lnx_b, nat_rpb, w_out, ple_norm_g, w_ple_gate, w_ple_proj, final_norm_g, _debug=False):
    f = lambda a: np.ascontiguousarray(np.asarray(a, dtype=np.float32))
    x = f(x); p = f(p)
    B, SEQ, _ = x.shape
    TS = SEQ // 4
    ROWS = SEQ // 64
    RPS = TS // 64
    key = (TS, _debug)
    if key not in _CACHE:
        _CACHE[key] = build_program(TS, _debug)
    nc = _CACHE[key]
    mup, mun = f(shift_mu_prev)[0], f(shift_mu_next)[0]
    w0_, w2_, a0_, a2_ = f(decay_w0)[0], f(decay_w2)[0], f(iclr_a0)[0], f(iclr_a2)[0]
    col = lambda v, n: np.ascontiguousarray(v.reshape(n, 128).T)
    gvec = np.stack([col(f(norm_mix_g)[0], 16), col(f(ple_norm_g)[0], 16), col(f(final_norm_g), 16)], axis=1)
    kvec = np.stack([col(f(k_k)[0], 8), col(f(k_a)[0], 8), col(f(r_k)[0].reshape(-1), 8)], axis=1)
    lnx = np.stack([f(lnx_w)[0], f(lnx_b)[0]], axis=0)
    rpb = f(nat_rpb)[0]
    common = {"w_in": f(w_in)[0], "w_out": f(w_out)[0], "w_gate": f(w_ple_gate)[0], "w_ple": f(w_ple_proj)[0],
              "gvec": np.ascontiguousarray(gvec), "kvec": np.ascontiguousarray(kvec), "lnx": lnx}
    in_maps = []
    for c in range(8):
        b, s = c // 4, c % 4
        xT = np.zeros((D, SEQ + 2), np.float32)
        xT[:, 1:SEQ + 1] = x[b].T
        units = [(i, False) for i in range(s)] + [(i, True) for i in range(3, s, -1)]
        keep = [0.0 if (ui == 0 or ui == s) else 1.0 for ui in range(3)]
        units = units + [(s, False), (s, True)]
        xu = np.empty((5, D, TS + 2), np.float32)
        muv = np.empty((128, 2, 5, 26), np.float32)
        w0a0 = np.empty((128, 2, 5, 8), np.float32)
        w2u = np.zeros((5, 128, 1024), np.float32)
        a2u = np.zeros((5, 128, 1024), np.float32)
        for ui, (seg, rev) in enumerate(units):
            blk = xT[:, seg * TS:seg * TS + TS + 2]
            xu[ui] = blk[:, ::-1] if rev else blk
            d_ = 1 if rev else 0
            muv[:, 0, ui, :] = col(mun if rev else mup, 26)
            muv[:, 1, ui, :] = col(mup if rev else mun, 26)
            w0a0[:, 0, ui, :] = col(w0_[d_], 8)
            w0a0[:, 1, ui, :] = col(a0_[d_], 8)
            w2u[ui, d_ * 64:(d_ + 1) * 64, :] = w2_[d_]
            a2u[ui, d_ * 64:(d_ + 1) * 64, :] = a2_[d_]
        flags = np.zeros((128, 16), np.float32)
        flags[:, 0:3] = np.asarray(keep, np.float32)[None, :]
        if s > 0:
            flags[:, 5 + s - 1] = 1.0
        if s < 3:
            flags[:, 8 + 2] = 1.0
        xnh = np.zeros((D, 512), np.float32)
        t0 = s * TS
        lo = max(t0 - 256, 0)
        xnh[:, 256 - (t0 - lo):256] = x[b, lo:t0].T
        hi = min(t0 + TS + 256, SEQ)
        xnh[:, 256:256 + (hi - (t0 + TS))] = x[b, t0 + TS:hi].T
        R0 = s * RPS
        bint = _nat_table(rpb, ROWS // 2, ROWS // 2 - 4, 4, ROWS)
        bedge = np.empty((7, 128, 16, 384), np.float32)
        for e in range(4):
            bedge[e] = _nat_table(rpb, R0 + e, R0 - 4, 6, ROWS)
        for e in range(3):
            bedge[4 + e] = _nat_table(rpb, R0 + RPS - 3 + e, R0 + RPS - 9, 6, ROWS)
        m = dict(common)
        m.update({"xu": xu, "xnh": xnh, "pT": np.ascontiguousarray(p[0, b, t0:t0 + TS].T), "muv": muv, "w0a0": w0a0,
                  "w2u": w2u, "a2u": a2u, "flags": flags, "bint": bint, "bedge": bedge})
        in_maps.append(m)
    res = run_bass_kernel_spmd(nc, in_maps, core_ids=list(range(8)))
    out = np.empty((B, SEQ, D), np.float32)
    for c in range(8):
        b, s = c // 4, c % 4
        out[b, s * TS:(s + 1) * TS, :] = np.asarray(res.results[c]["outT"]).T
    if _debug:
        return out, res
    return out
```

```python
import contextlib
import os
import math
import numpy as np
import concourse.bass as bass
import concourse.mybir as mybir
from concourse.bass_utils import run_bass_kernel_spmd

F32 = mybir.dt.float32
BF16 = mybir.dt.bfloat16
ALU = mybir.AluOpType
AF = mybir.ActivationFunctionType
AX = mybir.AxisListType

D = 2048
KC = 16
IN_W = 8448
NCC = 66
C = 64
TT = 256
TP = 512
NCH = TT // C
DSC = math.exp(-0.5)
NEG = -30000.0

ENGS = ("pe", "act", "dve", "pool", "sp")
STRICT_SAME_ENGINE = os.environ.get('KSTRICT', '1') == '1'
N_DMA_SEMS = 24


class Buf:
    __slots__ = ("name", "w", "r")

    def __init__(self, name):
        self.name = name
        self.w = None
        self.r = []


class Op:
    __slots__ = ("eng", "fn", "deps", "idx", "is_dma", "sig", "ev")

    def __init__(self, eng, fn, is_dma):
        self.eng = eng
        self.fn = fn
        self.deps = []
        self.is_dma = is_dma
        self.sig = False
        self.ev = None


class Sched:
    def __init__(self, nc):
        self.nc = nc
        self.ops = []
        self.last = {e: None for e in ENGS}

    def op(self, eng, fn, reads=(), writes=(), dma=False):
        o = Op(eng, fn, dma)
        deps = set()
        for b in reads:
            if b.w is not None:
                deps.add(b.w)
        for b in writes:
            if b.w is not None:
                deps.add(b.w)
            for r in b.r:
                deps.add(r)
        for b in reads:
            if not dma:
                b.r = [x for x in b.r if x.is_dma or x.eng != eng]
            b.r.append(o)
        for b in writes:
            b.w = o
            b.r = []
        deps.discard(o)
        o.deps = list(deps)
        o.idx = len(self.ops)
        self.ops.append(o)
        self.last[eng] = o
        return o

    def barrier(self):
        lasts = [o for o in self.last.values() if o is not None]
        for e in ENGS:
            o = Op(e, None, False)
            o.deps = list(lasts)
            o.idx = len(self.ops)
            self.ops.append(o)
            self.last[e] = o

    def finish(self, out_ops):
        o = Op("sp", None, False)
        o.deps = list(out_ops)
        o.idx = len(self.ops)
        self.ops.append(o)

    def emit(self, st):
        nc = self.nc
        engs = {"pe": nc.tensor, "act": nc.scalar, "dve": nc.vector, "pool": nc.gpsimd, "sp": nc.sync}
        sem = {e: st.enter_context(nc.semaphore("sem_" + e)) for e in ENGS}
        dsems = [st.enter_context(nc.semaphore("dsem%d" % i)) for i in range(N_DMA_SEMS)]
        for o in self.ops:
            for d in o.deps:
                if d.is_dma:
                    continue
                if d.eng != o.eng or o.is_dma:
                    d.sig = True
                elif STRICT_SAME_ENGINE and d.eng != "pe":
                    d.sig = True
        cnt = {e: 0 for e in ENGS}
        seen = {e: {} for e in ENGS}
        dcount = [0] * N_DMA_SEMS
        NSP = N_DMA_SEMS - 8
        dnx = {"sp": 0, "pool": 0, "act": 0}
        for o in self.ops:
            E = engs[o.eng]
            need = {}
            for d in o.deps:
                if d.ev is None:
                    continue
                key, val = d.ev
                if (not d.is_dma) and d.eng == o.eng and not o.is_dma:
                    if o.eng == "pe" or not STRICT_SAME_ENGINE:
                        continue
                if need.get(key, 0) < val:
                    need[key] = val
            if o.is_dma:
                if o.eng == "sp":
                    k = dnx["sp"] % NSP
                else:
                    k = NSP + dnx[o.eng] % 8
                dnx[o.eng] += 1
                if dcount[k] > 0:
                    key = ("d", k)
                    if need.get(key, 0) < dcount[k]:
                        need[key] = dcount[k]
            for key, val in need.items():
                if seen[o.eng].get(key, 0) >= val:
                    continue
                seen[o.eng][key] = val
                s = sem[key[1]] if key[0] == "e" else dsems[key[1]]
                E.wait_ge(s, val)
            if o.fn is None:
                continue
            ins = o.fn()
            if o.is_dma:
                dcount[k] += 16
                ins.then_inc(dsems[k], 16)
                o.ev = (("d", k), dcount[k])
            elif o.sig:
                cnt[o.eng] += 1
                ins.then_inc(sem[o.eng], 1)
                o.ev = (("e", o.eng), cnt[o.eng])


class TB:
    def __init__(self, t, name):
        self.t = t
        self.b = Buf(name)

    def __getitem__(self, k):
        return self.t[k]


def build_program(TS, debug=False):
    NT = TS // TT
    RPS = TS // 64
    TE = TS + 512
    nc = bass.Bass("TRN2", target_bir_lowering=False)
    dt_in = lambda name, shape: nc.dram_tensor(name, shape, F32, kind="ExternalInput").ap()
    xu = dt_in("xu", [5, D, TS + 2])
    xnh = dt_in("xnh", [D, 512])
    pT = dt_in("pT", [256, TS])
    w_in = dt_in("w_in", [D, IN_W])
    w_out = dt_in("w_out", [D, D])
    w_gate = dt_in("w_gate", [D, D])
    w_ple = dt_in("w_ple", [256, D])
    gvec = dt_in("gvec", [128, 3, KC])
    muv = dt_in("muv", [128, 2, 5, 26])
    w0a0 = dt_in("w0a0", [128, 2, 5, 8])
    w2u = dt_in("w2u", [5, 128, 1024])
    a2u = dt_in("a2u", [5, 128, 1024])
    kvec = dt_in("kvec", [128, 3, 8])
    lnx = dt_in("lnx", [2, 1024])
    flags = dt_in("flags", [128, 16])
    bint = dt_in("bint", [128, 16, 256])
    bedge = dt_in("bedge", [7, 128, 16, 384])
    outT = nc.dram_tensor("outT", [D, TS], F32, kind="ExternalOutput").ap()
    kindS = "ExternalOutput" if debug else "Internal"
    scr = lambda name, shape, dt: nc.dram_tensor(name, shape, dt, kind=(kindS if name in ("MIX", "YB", "QT", "KTn", "GT", "VN") else "Internal")).ap()
    wi_bf = scr("wi_bf", [D, IN_W], BF16)
    wo_bf = scr("wo_bf", [D, D], BF16)
    wg_bf = scr("wg_bf", [D, D], BF16)
    wp_bf = scr("wp_bf", [256, D], BF16)
    YB = scr("YB", [TS, 1024], F32)
    MIX = scr("MIX", [D, TS], BF16)
    QT = scr("QT", [1024, TS], BF16)
    KTn = scr("KTn", [1024, TE], BF16)
    GT = scr("GT", [1024, TS], BF16)
    VN = scr("VN", [TE, 1024], BF16)
    HSd = scr("HSd", [3, 128, 1024], F32)
    bHS = [Buf("HS%d" % i) for i in range(3)]
    bYB, bMIX, bQT, bKTn, bGT, bVN = (Buf(n) for n in ("YB", "MIX", "QT", "KTn", "GT", "VN"))
    bwi, bwo, bwg, bwp = (Buf(n) for n in ("wi", "wo", "wg", "wp"))

    S = Sched(nc)
    T, V, A, P = nc.tensor, nc.vector, nc.scalar, nc.gpsimd
    EV = {"dve": V, "pool": P}

    def mm(out, lhsT, rhs, start, stop, r, w):
        S.op("pe", lambda: T.matmul(out, lhsT, rhs, start=start, stop=stop), r, w)

    def tt(eng, out, a, b, op, r, w):
        S.op(eng, lambda: EV[eng].tensor_tensor(out, a, b, op), r, w)

    def tsc(eng, out, a, s1, s2, op0, op1, r, w):
        if s2 is None:
            S.op(eng, lambda: EV[eng].tensor_scalar(out, a, s1, None, op0), r, w)
        else:
            S.op(eng, lambda: EV[eng].tensor_scalar(out, a, s1, s2, op0, op1), r, w)

    def stt(eng, out, a, s, b, op0, op1, r, w):
        S.op(eng, lambda: EV[eng].scalar_tensor_tensor(out, a, s, b, op0, op1), r, w)

    def act(out, in_, func, r, w, bias=None, scale=1.0):
        if bias is None:
            S.op("act", lambda: A.activation(out=out, in_=in_, func=func, scale=scale), r, w)
        else:
            S.op("act", lambda: A.activation(out=out, in_=in_, func=func, bias=bias, scale=scale), r, w)

    def cp(eng, out, in_, r, w):
        if eng == "act":
            act(out, in_, AF.Copy, r, w)
        else:
            S.op(eng, lambda: EV[eng].tensor_copy(out, in_), r, w)

    def mset(eng, ap, val, w):
        S.op(eng, lambda: EV[eng].memset(ap, val), (), w)

    def dma(q, out, in_, r, w):
        e = {"sp": nc.sync, "pool": nc.gpsimd, "act": nc.scalar}[q]
        return S.op(q, lambda: e.dma_start(out=out, in_=in_), r, w, dma=True)

    out_stores = []
    with contextlib.ExitStack() as st0:
        def sb(st, name, shape, dt=F32):
            return TB(st.enter_context(nc.sbuf_tensor(name, shape, dt)), name)

        pb = [TB(st0.enter_context(nc.psum_tensor("pb%d" % i, [128, 512], F32)), "pb%d" % i) for i in range(8)]
        ident = sb(st0, "ident", [128, 128], BF16)
        identf = sb(st0, "identf", [128, 128], F32)
        jmat = sb(st0, "jmat", [64, 64], F32)
        onesb = sb(st0, "onesb", [128, 128], BF16)
        blk1 = sb(st0, "blk1", [128, 128], BF16)
        hsel = sb(st0, "hsel", [128, 2], BF16)
        gv = sb(st0, "gv", [128, 3, KC])
        mu = sb(st0, "mu", [128, 2, 5, 26])
        mu0 = sb(st0, "mu0", [128, 5, 26])
        w0 = sb(st0, "w0", [128, 2, 5, 8])
        kv = sb(st0, "kv", [128, 3, 8])
        fl = sb(st0, "fl", [128, 16])
        cmask = sb(st0, "cmask", [128, TT])
        msk_s = sb(st0, "msk_s", [64, 512])
        msk_i = sb(st0, "msk_i", [64, 512])
        msk_l = sb(st0, "msk_l", [64, 512])
        id8 = sb(st0, "id8", [64, 512])
        epsc = sb(st0, "epsc", [128, 1])
        S.op("pool", lambda: P.memset(identf[:], 0.0), (), [identf.b])
        iot_p = sb(st0, "iot_p", [128, 1])
        iot_f = sb(st0, "iot_f", [128, 128])
        S.op("pool", lambda: P.iota(iot_f[:], pattern=[[1, 128]], base=0, channel_multiplier=0, allow_small_or_imprecise_dtypes=True), (), [iot_f.b])
        S.op("pool", lambda: P.iota(iot_p[:], pattern=[[0, 1]], base=0, channel_multiplier=1, allow_small_or_imprecise_dtypes=True), (), [iot_p.b])
        tsc("dve", identf[:], iot_f[:], iot_p[:, 0:1], None, ALU.is_equal, None, [iot_f.b, iot_p.b], [identf.b])
        cp("dve", ident[:], identf[:], [identf.b], [ident.b])
        pj = sb(st0, "pj", [128, 1])
        tsc("dve", pj[:], iot_p[:], -1.0, 63.0, ALU.mult, ALU.add, [iot_p.b], [pj.b])
        tsc("dve", jmat[:], iot_f[0:64, 0:64], pj[0:64, 0:1], None, ALU.is_equal, None, [iot_f.b, pj.b], [jmat.b])
        mset("dve", onesb[:], 1.0, [onesb.b])
        mset("dve", blk1[:], 0.0, [blk1.b])
        mset("dve", blk1[0:64, 0:64], 1.0, [blk1.b])
        mset("dve", blk1[64:128, 64:128], 1.0, [blk1.b])
        mset("dve", hsel[:], 0.0, [hsel.b])
        mset("dve", hsel[0:64, 0:1], 1.0, [hsel.b])
        mset("dve", hsel[64:128, 1:2], 1.0, [hsel.b])
        mset("dve", epsc[:], 1e-6, [epsc.b])
        epsl = sb(st0, "epsl", [128, 1])
        onef = sb(st0, "onef", [128, 1])
        mset("dve", onef[:], 1.0, [onef.b])
        mset("dve", epsl[:], 64e-5, [epsl.b])
        mset("dve", cmask[:], 1.0, [cmask.b])
        mset("dve", cmask[:].rearrange("p (c k) -> p c k", k=C)[:, :, 0:1], 0.0, [cmask.b])
        tsc("dve", msk_s[:, 0:64], iot_f[0:64, 0:64], iot_p[0:64, 0:1], None, ALU.is_gt, None, [iot_f.b, iot_p.b], [msk_s.b])
        tsc("dve", msk_i[:, 0:64], iot_f[0:64, 0:64], iot_p[0:64, 0:1], None, ALU.is_ge, None, [iot_f.b, iot_p.b], [msk_i.b])
        tsc("dve", msk_l[:, 0:64], iot_f[0:64, 0:64], iot_p[0:64, 0:1], None, ALU.is_lt, None, [iot_f.b, iot_p.b], [msk_l.b])
        cp("dve", id8[:, 0:64], identf[0:64, 0:64], [identf.b], [id8.b])
        for m_ in (msk_s, msk_i, msk_l, id8):
            for rr in range(1, 8):
                cp("dve", m_[:, rr * 64:(rr + 1) * 64], m_[:, 0:64], [m_.b], [m_.b])
        dma("sp", gv[:], gvec, [], [gv.b])
        dma("sp", mu[:], muv, [], [mu.b])
        dma("sp", w0[:], w0a0, [], [w0.b])
        dma("sp", kv[:], kvec, [], [kv.b])
        dma("sp", fl[:], flags, [], [fl.b])
        tt("dve", mu0[:], mu[:, 0], mu[:, 1], ALU.add, [mu.b], [mu0.b])
        tsc("dve", mu0[:], mu0[:], -1.0, 1.0, ALU.mult, ALU.add, [mu0.b], [mu0.b])

        with contextlib.ExitStack() as st:
            wl = [sb(st, "wl%d" % i, [128, 2112]) for i in range(2)]
            wc = [sb(st, "wc%d" % i, [128, 2112], BF16) for i in range(2)]
            it = 0
            jobs = []
            for kc in range(KC):
                for q in range(4):
                    jobs.append((w_in[kc * 128:(kc + 1) * 128, q * 2112:(q + 1) * 2112], wi_bf[kc * 128:(kc + 1) * 128, q * 2112:(q + 1) * 2112], gv[:, 0, kc:kc + 1], 2112, bwi))
            for kc in range(KC):
                jobs.append((w_out[kc * 128:(kc + 1) * 128, :], wo_bf[kc * 128:(kc + 1) * 128, :], None, 2048, bwo))
                jobs.append((w_gate[kc * 128:(kc + 1) * 128, :], wg_bf[kc * 128:(kc + 1) * 128, :], gv[:, 1, kc:kc + 1], 2048, bwg))
            for kc in range(2):
                jobs.append((w_ple[kc * 128:(kc + 1) * 128, :], wp_bf[kc * 128:(kc + 1) * 128, :], None, 2048, bwp))
            for (src, dst, sc, n, bdst) in jobs:
                a, b_ = wl[it % 2], wc[it % 2]
                dma("sp", a[:, 0:n], src, [], [a.b])
                if sc is None:
                    if it % 2 == 0:
                        cp("act", b_[:, 0:n], a[:, 0:n], [a.b], [b_.b])
                    else:
                        cp("dve", b_[:, 0:n], a[:, 0:n], [a.b], [b_.b])
                else:
                    if it % 2 == 0:
                        act(b_[:, 0:n], a[:, 0:n], AF.Copy, [a.b, gv.b], [b_.b], scale=sc)
                    else:
                        tsc("dve", b_[:, 0:n], a[:, 0:n], sc, None, ALU.mult, None, [a.b, gv.b], [b_.b])
                dma("pool", dst, b_[:, 0:n], [b_.b], [bdst])
                it += 1
        S.barrier()

        def load_x_tile(st_bufs, src_ap, ncols):
            xs, xb, sq, rs = st_bufs
            half = KC // 2
            v = src_ap.rearrange("(kc p) n -> p kc n", p=128)
            for hf in range(2):
                dma("sp", xs[:, :, 0:ncols], v[:, hf * half:(hf + 1) * half, :], [], [xs.b])
                cp("dve", xb[:, hf * half:(hf + 1) * half, 0:ncols], xs[:, :, 0:ncols], [xs.b], [xb.b])
                act(sq[:, :, 0:ncols], xs[:, :, 0:ncols], AF.Square, [xs.b], [sq.b])
                for k8 in range(half):
                    kc = hf * half + k8
                    mm(pb[7][:, 0:ncols], onesb[:], sq[:, k8, 0:ncols], kc == 0, kc == KC - 1, [onesb.b, sq.b], [pb[7].b])
            act(rs[:, 0:ncols], pb[7][:, 0:ncols], AF.Sqrt, [pb[7].b, epsc.b], [rs.b], bias=epsc[:, 0:1], scale=1.0 / D)
            S.op("dve", lambda: V.reciprocal(rs[:, 0:ncols], rs[:, 0:ncols]), [rs.b], [rs.b])

        def load_w_group(wt, src_bf, col0, ncols, bsrc, nk=KC):
            v = src_bf.rearrange("(kc p) n -> p kc n", p=128)
            dma("sp", wt[:, 0:nk, 0:ncols], v[:, :, col0:col0 + ncols], [bsrc], [wt.b])

        DO = os.environ.get('KDO', 'rnp')
        UNITS = os.environ.get('KUNITS', '01243h')
        with contextlib.ExitStack() as st:
            xs = sb(st, "xs", [128, KC // 2, TT + 2])
            xb = sb(st, "xb", [128, KC, TT + 2], BF16)
            sq = sb(st, "sq", [128, KC // 2, TT + 2], BF16)
            rs = sb(st, "rs", [128, TT + 2])
            wt = [sb(st, "wt%d" % i, [128, KC, 256], BF16) for i in range(2)]
            zr = [sb(st, "zr%d" % i, [128, TT + 2]) for i in range(2)]
            ztmp = [sb(st, "ztmp%d" % i, [128, TT]) for i in range(2)]
            Rf = sb(st, "Rf", [128, 8, TT], BF16)
            Kf = sb(st, "Kf", [128, 8, TT])
            wdf = sb(st, "wdf", [128, TT], BF16)
            adf = sb(st, "adf", [128, TT], BF16)
            w2b = sb(st, "w2b", [128, 1024], BF16)
            a2b = sb(st, "a2b", [128, 1024], BF16)
            w2f = sb(st, "w2f", [128, 1024])
            AR = sb(st, "AR", [128, 8, NCH, 2, C], BF16)
            KTt = sb(st, "KTt", [128, 8, TT], BF16)
            BTt = sb(st, "BTt", [128, 8, TT], BF16)
            KH = sb(st, "KH", [128, 8, TT], BF16)
            BH = sb(st, "BH", [128, 8, TT], BF16)
            VB = sb(st, "VB", [128, 8, TT], BF16)
            RK = sb(st, "RK", [128, 8, TT], BF16)
            SG = sb(st, "SG", [128, 8, TT], BF16)
            PC = sb(st, "PC", [128, 8, NCH])
            tmp = [sb(st, "tmp%d" % i, [128, TT]) for i in range(8)]
            tmpb = sb(st, "tmpb", [128, TT], BF16)
            KHt = [[sb(st, "KHt%d%d" % (p_, h_), [64, 512], BF16) for h_ in range(2)] for p_ in range(2)]
            BHt = [[sb(st, "BHt%d%d" % (p_, h_), [64, 512], BF16) for h_ in range(2)] for p_ in range(2)]
            Vt = [[sb(st, "Vt%d%d" % (p_, h_), [64, 512], BF16) for h_ in range(2)] for p_ in range(2)]
            hb = lambda nm: [sb(st, "%s%d" % (nm, h_), [64, 512], BF16) for h_ in range(2)]
            aMt = hb("aMt")
            aM = hb("aM")
            aAak = [hb("aAak0"), hb("aAak1")]
            aArb = [hb("aArb0"), hb("aArb1")]
            aArk = [hb("aArk0"), hb("aArk1")]
            Tt = [hb("Tt0"), hb("Tt1")]
            Pk = [[sb(st, "Pk%d%d" % (h_, i), [64, 512], BF16) for i in range(2)] for h_ in range(2)]
            Qk = [[sb(st, "Qk%d%d" % (h_, i), [64, 512], BF16) for i in range(2)] for h_ in range(2)]
            Rm = [sb(st, "Rm%d" % h_, [64, 512]) for h_ in range(2)]
            Rb = [[sb(st, "Rb%d%d" % (h_, i), [64, 512], BF16) for i in range(2)] for h_ in range(2)]
            Xs = hb("Xs")
            Us = hb("Us")
            Ys = sb(st, "Ys", [64, 1024])
            Hm = sb(st, "Hm", [128, 8, 128])
            Hb = sb(st, "Hb", [128, 8, 128], BF16)
            ybl = sb(st, "ybl", [64, 1024])
            gn = [sb(st, "gn%d" % i, [64, 1024]) for i in range(1)]
            gs = [sb(st, "gs%d" % i, [64, 16]) for i in range(4)]
            lnw = sb(st, "lnw", [64, 1024])
            lnb = sb(st, "lnb", [64, 1024])
            yab = sb(st, "yab", [64, 1024], BF16)
            mixt = sb(st, "mixt", [128, 8, TT], BF16)
            natq = sb(st, "natq", [128, TT], BF16)
            vtok = sb(st, "vtok", [128, 512], BF16)
            rtk = sb(st, "rtk", [128, 4])
            dma("sp", lnw[:], lnx[0:1, :].partition_broadcast(64), [], [lnw.b])
            dma("sp", lnb[:], lnx[1:2, :].partition_broadcast(64), [], [lnb.b])
            mset("dve", Hm[:], 0.0, [Hm.b])
            mset("dve", Hb[:], 0.0, [Hb.b])
            xbufs = (xs, xb, sq, rs)
            wcount = [0]

            def zchunks(cc_list, consumer, halo):
                gi = 0
                while gi < len(cc_list):
                    grp = cc_list[gi:gi + 2]
                    contiguous = all(grp[i] == grp[0] + i for i in range(len(grp)))
                    assert contiguous
                    w = wt[wcount[0] % 2]
                    wcount[0] += 1
                    load_w_group(w, wi_bf, grp[0] * 128, len(grp) * 128, bwi)
                    for li, cc in enumerate(grp):
                        pbm = pb[cc % 2]
                        z = zr[cc % 2]
                        c_lo, c_hi = (0, TT + 2) if halo else (1, TT + 1)
                        for kc in range(KC):
                            mm(pbm[:, c_lo:c_hi], w[:, kc, li * 128:(li + 1) * 128], xb[:, kc, c_lo:c_hi], kc == 0, kc == KC - 1, [w.b, xb.b], [pbm.b])
                        tt("dve", z[:, c_lo:c_hi], pbm[:, c_lo:c_hi], rs[:, c_lo:c_hi], ALU.mult, [pbm.b, rs.b], [z.b])
                        consumer(cc, z)
                    gi += 2

            def shift_into(u, cc, z, dst_ap, dstb):
                t1 = ztmp[cc % 2]
                act(t1[:], z[:, 1:TT + 1], AF.Copy, [z.b, mu0.b], [t1.b], scale=mu0[:, u, cc:cc + 1])
                stt("dve", t1[:], z[:, 0:TT], mu[:, 0, u, cc:cc + 1], t1[:], ALU.mult, ALU.add, [z.b, mu.b, t1.b], [t1.b])
                stt("dve", dst_ap, z[:, 2:TT + 2], mu[:, 1, u, cc:cc + 1], t1[:], ALU.mult, ALU.add, [z.b, mu.b, t1.b], [dstb])

            def run_unit(u, full, own):
                dma("sp", w2f[:], w2u[u], [], [w2f.b])
                cp("dve", w2b[:], w2f[:], [w2f.b], [w2b.b])
                dma("sp", w2f[:], a2u[u], [], [w2f.b])
                cp("dve", a2b[:], w2f[:], [w2f.b], [a2b.b])
                for j in range(NT):
                    load_x_tile(xbufs, xu[u][:, j * TT:j * TT + TT + 2], TT + 2)

                    def consumer(cc, z):
                        if cc < 8:
                            shift_into(u, cc, z, Rf[:, cc, :], Rf.b)
                        elif cc < 16:
                            shift_into(u, cc, z, Kf[:, cc - 8, :], Kf.b)
                        elif cc < 24:
                            shift_into(u, cc, z, VB[:, cc - 16, :], VB.b)
                        elif cc == 24:
                            shift_into(u, cc, z, tmp[0][:], tmp[0].b)
                            act(wdf[:], tmp[0][:], AF.Tanh, [tmp[0].b], [wdf.b])
                        elif cc == 25:
                            shift_into(u, cc, z, adf[:], adf.b)
                        elif cc < 34:
                            act(SG[:, cc - 26, :], z[:, 1:TT + 1], AF.Silu, [z.b], [SG.b])

                    cols = list(range(8, 26)) if not full else list(range(0, 26))
                    if not full:
                        zchunks(cols, consumer, True)
                    else:
                        zchunks(list(range(0, 26)), consumer, True)
                        if own:
                            zchunks(list(range(26, 34)), consumer, False)
                    if 'p' not in os.environ.get('KSTAGE', 'ps'):
                        continue
                    for ct in range(8):
                        sg, ic, cs, kk, t4, t5, t6, t7 = tmp
                        mm(pb[0][:, 0:TT], w2b[:, ct * 128:(ct + 1) * 128], wdf[:], True, True, [w2b.b, wdf.b], [pb[0].b])
                        act(sg[:], pb[0][:, 0:TT], AF.Sigmoid, [pb[0].b, w0.b], [sg.b], bias=w0[:, 0, u, ct:ct + 1])
                        mm(pb[1][:, 0:TT], a2b[:, ct * 128:(ct + 1) * 128], adf[:], True, True, [a2b.b, adf.b], [pb[1].b])
                        act(ic[:], pb[1][:, 0:TT], AF.Sigmoid, [pb[1].b, w0.b], [ic.b], bias=w0[:, 1, u, ct:ct + 1])
                        S.op("dve", lambda: V.tensor_tensor_scan(cs[:], cmask[:], sg[:], 0.0, ALU.mult, ALU.add), [cmask.b, sg.b], [cs.b])
                        tsc("dve", kk[:], Kf[:, ct, :], kv[:, 0, ct:ct + 1], None, ALU.mult, None, [Kf.b, kv.b], [kk.b])
                        tt("dve", tmpb[:], kk[:], kk[:], ALU.mult, [kk.b], [tmpb.b])
                        mm(pb[2][:, 0:TT], blk1[:], tmpb[:], True, True, [blk1.b, tmpb.b], [pb[2].b])
                        act(t4[:], pb[2][:, 0:TT], AF.Sqrt, [pb[2].b], [t4.b])
                        tsc("dve", t4[:], t4[:], 1e-12, None, ALU.max, None, [t4.b], [t4.b])
                        S.op("dve", lambda: V.reciprocal(t4[:], t4[:]), [t4.b], [t4.b])
                        tt("dve", kk[:], kk[:], t4[:], ALU.mult, [kk.b, t4.b], [kk.b])
                        tt("dve", t5[:], kk[:], ic[:], ALU.mult, [kk.b, ic.b], [t5.b])
                        tsc("dve", t6[:], ic[:], -1.0, kv[:, 1, ct:ct + 1], ALU.add, ALU.mult, [ic.b, kv.b], [t6.b])
                        stt("dve", t6[:], t6[:], 1.0, Kf[:, ct, :], ALU.add, ALU.mult, [t6.b, Kf.b], [t6.b])
                        act(t4[:], cs[:], AF.Exp, [cs.b], [t4.b], scale=-DSC)
                        cp("dve", PC[:, ct, :], t4[:].rearrange("p (c k) -> p c k", k=C)[:, :, C - 1], [t4.b], [PC.b])
                        arv = AR[:, ct].rearrange("p c two k -> p two c k")
                        tt("dve", arv[:, 1], Rf[:, ct, :].rearrange("p (c k) -> p c k", k=C), t4[:].rearrange("p (c k) -> p c k", k=C), ALU.mult, [Rf.b, t4.b], [AR.b])
                        tt("dve", t7[:], cs[:], sg[:], ALU.subtract, [cs.b, sg.b], [t7.b])
                        act(t7[:], t7[:], AF.Exp, [t7.b], [t7.b], scale=-DSC)
                        stt("dve", arv[:, 0], kk[:].rearrange("p (c k) -> p c k", k=C), -1.0, t7[:].rearrange("p (c k) -> p c k", k=C), ALU.mult, ALU.mult, [kk.b, t7.b], [AR.b])
                        act(t4[:], cs[:], AF.Exp, [cs.b], [t4.b], scale=DSC)
                        tt("dve", KTt[:, ct, :], t6[:], t4[:], ALU.mult, [t6.b, t4.b], [KTt.b])
                        tt("dve", BTt[:, ct, :], t5[:], t4[:], ALU.mult, [t5.b, t4.b], [BTt.b])
                        tt("dve", t4[:].rearrange("p (c k) -> p c k", k=C), t4[:].rearrange("p (c k) -> p c k", k=C),
                           PC[:, ct, :].unsqueeze(2).to_broadcast([128, NCH, C]), ALU.mult, [t4.b, PC.b], [t4.b])
                        tt("dve", KH[:, ct, :], t6[:], t4[:], ALU.mult, [t6.b, t4.b], [KH.b])
                        tt("dve", BH[:, ct, :], t5[:], t4[:], ALU.mult, [t5.b, t4.b], [BH.b])
                        if own:
                            stt("dve", RK[:, ct, :], Rf[:, ct, :], kv[:, 2, ct:ct + 1], Kf[:, ct, :], ALU.mult, ALU.mult, [Rf.b, kv.b, Kf.b], [RK.b])
                    if 's' not in os.environ.get('KSTAGE', 'ps'):
                        continue
                    def off_tok(n, p):
                        csl = slice(n * C, (n + 1) * C)
                        for qi, (src, dstT) in enumerate(((KH, KHt), (BH, BHt), (VB, Vt))):
                            for half in range(2):
                                pbt = pb[(qi * 2 + half) % 4]
                                for q in range(4):
                                    ct = half * 4 + q
                                    mm(pbt[0:64, q * 128:(q + 1) * 128], src[:, ct, csl], ident[:], True, True, [src.b, ident.b], [pbt.b])
                                d_ = dstT[p][half]
                                cp("act" if half == 0 else "dve", d_[:], pbt[0:64, :], [pbt.b], [d_.b])

                    def off_A(n, p):
                        csl = slice(n * C, (n + 1) * C)
                        for half in range(2):
                            specs = ((BTt, 0, aMt[half], msk_s, 0, 0), (BTt, 1, aArb[p][half], msk_i, 0, 1), (KTt, 0, aAak[p][half], msk_s, 1, 0), (KTt, 1, aArk[p][half], msk_i, 1, 1))
                            for (lsrc, ar_i, dst, msk, bi, slot) in specs:
                                if (not full) and ar_i == 1:
                                    continue
                                for e in range(2):
                                    pbx = pb[bi + 2 * e]
                                    ps_ = slice(e * 64, e * 64 + 64)
                                    for qq in range(4):
                                        ct = half * 4 + qq
                                        mm(pbx[0:64, slot * 256 + qq * 64:slot * 256 + (qq + 1) * 64], lsrc[ps_, ct, csl], AR[ps_, ct, n, ar_i, :], True, True, [lsrc.b, AR.b], [pbx.b])
                                    tt("dve", dst[:].rearrange("p (q e k) -> p q e k", e=2, k=64)[:, :, e, :], pbx[0:64, slot * 256:(slot + 1) * 256].rearrange("p (q k) -> p q k", k=64),
                                       msk[:, 0:256].rearrange("p (q k) -> p q k", k=64), ALU.mult, [pbx.b, msk.b], [dst.b])
                            for e in range(2):
                                pbx = pb[2 * e]
                                ps_ = slice(e * 64, e * 64 + 64)
                                for qq in range(4):
                                    ct = half * 4 + qq
                                    mm(pbx[0:64, qq * 64:(qq + 1) * 64], AR[ps_, ct, n, 0, :], BTt[ps_, ct, csl], True, True, [BTt.b, AR.b], [pbx.b])
                                tt("dve", aM[half][:].rearrange("p (q e k) -> p q e k", e=2, k=64)[:, :, e, :], pbx[0:64, 0:256].rearrange("p (q k) -> p q k", k=64),
                                   msk_l[:, 0:256].rearrange("p (q k) -> p q k", k=64), ALU.mult, [pbx.b, msk_l.b], [aM[half].b])
                            tt("dve", Rm[half][:], aMt[half][:], id8[:], ALU.add, [aMt[half].b, id8.b], [Rm[half].b])
                            cp("act", Rb[half][0][:], Rm[half][:], [Rm[half].b], [Rb[half][0].b])

                    def off_lev(n, p, lev):
                        for half in range(2):
                            Pc, Qc = (aM[half], aMt[half]) if lev == 1 else (Pk[half][(lev - 1) % 2], Qk[half][(lev - 1) % 2])
                            Pn, Qn = Pk[half][lev % 2], Qk[half][lev % 2]
                            bP, bQ = pb[2 * half], pb[2 * half + 1]
                            for q in range(8):
                                sl = slice(q * 64, (q + 1) * 64)
                                mm(bP[0:64, sl], Qc[:, sl], Pc[:, sl], True, True, [Qc.b, Pc.b], [bP.b])
                            cp("act", Pn[:], bP[0:64, :], [bP.b], [Pn.b])
                            if lev < 5:
                                for q in range(8):
                                    sl = slice(q * 64, (q + 1) * 64)
                                    mm(bQ[0:64, sl], Pc[:, sl], Qc[:, sl], True, True, [Qc.b, Pc.b], [bQ.b])
                                cp("dve", Qn[:], bQ[0:64, :], [bQ.b], [Qn.b])
                        for half in range(2):
                            Pn = Pk[half][lev % 2]
                            Rcur = Rb[half][(lev - 1) % 2]
                            Rnext = Tt[p][half] if lev == 5 else Rb[half][lev % 2]
                            bP = pb[2 * half]
                            for q in range(8):
                                sl = slice(q * 64, (q + 1) * 64)
                                mm(bP[0:64, sl], Pn[:, sl], Rcur[:, sl], True, True, [Pn.b, Rcur.b], [bP.b])
                            tt("dve", Rm[half][:], Rm[half][:], bP[0:64, :], ALU.add, [Rm[half].b, bP.b], [Rm[half].b])
                            cp("act", Rnext[:], Rm[half][:], [Rm[half].b], [Rnext.b])

                    def on_X(n, p):
                        for half in range(2):
                            pbx = pb[4 + half]
                            for q in range(4):
                                ct = half * 4 + q
                                mm(pbx[0:64, q * 128:(q + 1) * 128], AR[:, ct, n, 0, :], Hb[:, ct, :], True, False, [AR.b, Hb.b], [pbx.b])
                                for e in range(2):
                                    hh = q * 2 + e
                                    mm(pbx[0:64, q * 128 + e * 64:q * 128 + (e + 1) * 64], aAak[p][half][:, hh * 64:(hh + 1) * 64], Vt[p][half][:, hh * 64:(hh + 1) * 64], False, e == 1,
                                       [aAak[p][half].b, Vt[p][half].b], [pbx.b])
                            cp("act" if half == 0 else "dve", Xs[half][:], pbx[0:64, :], [pbx.b], [Xs[half].b])

                    def on_U(n, p):
                        for half in range(2):
                            pbx = pb[6 + half]
                            for q in range(8):
                                sl = slice(q * 64, (q + 1) * 64)
                                mm(pbx[0:64, sl], Tt[p][half][:, sl], Xs[half][:, sl], True, True, [Tt[p][half].b, Xs[half].b], [pbx.b])
                            cp("act" if half == 0 else "dve", Us[half][:], pbx[0:64, :], [pbx.b], [Us[half].b])

                    def on_Y(n, p):
                        if not full:
                            return
                        for half in range(2):
                            pbx = pb[4 + half]
                            for q in range(4):
                                ct = half * 4 + q
                                mm(pbx[0:64, q * 128:(q + 1) * 128], AR[:, ct, n, 1, :], Hb[:, ct, :], True, False, [AR.b, Hb.b], [pbx.b])
                                for e in range(2):
                                    hh = q * 2 + e
                                    sl = slice(hh * 64, (hh + 1) * 64)
                                    o_ = pbx[0:64, q * 128 + e * 64:q * 128 + (e + 1) * 64]
                                    mm(o_, aArb[p][half][:, sl], Us[half][:, sl], False, False, [aArb[p][half].b, Us[half].b], [pbx.b])
                                    mm(o_, aArk[p][half][:, sl], Vt[p][half][:, sl], False, e == 1, [aArk[p][half].b, Vt[p][half].b], [pbx.b])
                            cp("act" if half == 0 else "dve", Ys[:, half * 512:(half + 1) * 512], pbx[0:64, :], [pbx.b], [Ys.b])

                    def on_H(n, p):
                        for half in range(2):
                            pbx = pb[6 + half]
                            for q in range(4):
                                sl = slice(q * 128, (q + 1) * 128)
                                mm(pbx[:, sl], BHt[p][half][:, sl], Us[half][:, sl], True, False, [BHt[p][half].b, Us[half].b], [pbx.b])
                                mm(pbx[:, sl], KHt[p][half][:, sl], Vt[p][half][:, sl], False, True, [KHt[p][half].b, Vt[p][half].b], [pbx.b])
                            for e in range(2):
                                pp = slice(e * 64, e * 64 + 64)
                                hv = Hm[pp, half * 4:(half + 1) * 4, e * 64:(e + 1) * 64]
                                pcv = PC[pp, half * 4:(half + 1) * 4, n:n + 1].to_broadcast([64, 4, 64])
                                tt("dve", hv, hv, pcv, ALU.mult, [Hm.b, PC.b], [Hm.b])
                                pv = pbx[pp, :].rearrange("p (q c) -> p q c", c=128)[:, :, e * 64:(e + 1) * 64]
                                tt("dve", hv, hv, pv, ALU.add, [Hm.b, pbx.b], [Hm.b])
                                cp("act", Hb[pp, half * 4:(half + 1) * 4, e * 64:(e + 1) * 64], hv, [Hm.b], [Hb.b])

                    off_tok(0, 0)
                    off_A(0, 0)
                    for lev in range(1, 6):
                        off_lev(0, 0, lev)
                    for n in range(NCH):
                        csl = slice(n * C, (n + 1) * C)
                        p = n % 2
                        nx = n + 1 < NCH
                        if nx:
                            off_tok(n + 1, 1 - p)
                            off_A(n + 1, 1 - p)
                            off_lev(n + 1, 1 - p, 1)
                        on_X(n, p)
                        if nx:
                            off_lev(n + 1, 1 - p, 2)
                        on_U(n, p)
                        if nx:
                            off_lev(n + 1, 1 - p, 3)
                        on_Y(n, p)
                        if nx:
                            off_lev(n + 1, 1 - p, 4)
                        on_H(n, p)
                        if nx:
                            off_lev(n + 1, 1 - p, 5)
                        Vtf = Vt[p]
                        if full and not own:
                            gchunk = j * NCH + n
                            nat0 = TS - (gchunk + 1) * C
                            for half in range(2):
                                pbx = pb[4 + half]
                                mm(pbx[0:64, :], jmat[:], Ys[:, half * 512:(half + 1) * 512], True, True, [jmat.b, Ys.b], [pbx.b])
                                cp("act" if half == 0 else "dve", ybl[:, half * 512:(half + 1) * 512], pbx[0:64, :], [pbx.b], [ybl.b])
                            dma("pool", YB[nat0:nat0 + C, :], ybl[:], [ybl.b], [bYB])
                        if full and own:
                            t0 = j * TT + n * C
                            dma("sp", ybl[:], YB[t0:t0 + C, :], [bYB], [ybl.b])
                            y2 = gn[0]
                            tt("dve", ybl[:], Ys[:], ybl[:], ALU.add, [Ys.b, ybl.b], [ybl.b])
                            y3 = ybl[:].rearrange("p (h k) -> p h k", k=64)
                            S.op("dve", lambda: V.tensor_reduce(gs[0][:], y3, AX.X, ALU.add), [ybl.b], [gs[0].b])
                            tsc("dve", gs[0][:], gs[0][:], 1.0 / 64, None, ALU.mult, None, [gs[0].b], [gs[0].b])
                            tt("dve", y3, y3, gs[0][:].unsqueeze(2).to_broadcast([64, 16, 64]), ALU.subtract, [ybl.b, gs[0].b], [ybl.b])
                            act(y2[:], ybl[:], AF.Square, [ybl.b], [y2.b])
                            y23 = y2[:].rearrange("p (h k) -> p h k", k=64)
                            S.op("dve", lambda: V.tensor_reduce(gs[1][:], y23, AX.X, ALU.add), [y2.b], [gs[1].b])
                            act(gs[1][:], gs[1][:], AF.Sqrt, [gs[1].b, epsl.b], [gs[1].b], bias=epsl[0:64, 0:1], scale=1.0 / 64)
                            S.op("dve", lambda: V.reciprocal(gs[1][:], gs[1][:]), [gs[1].b], [gs[1].b])
                            tt("dve", y3, y3, gs[1][:].unsqueeze(2).to_broadcast([64, 16, 64]), ALU.mult, [ybl.b, gs[1].b], [ybl.b])
                            tt("dve", ybl[:], ybl[:], lnw[:], ALU.mult, [ybl.b, lnw.b], [ybl.b])
                            tt("dve", ybl[:], ybl[:], lnb[:], ALU.add, [ybl.b, lnb.b], [ybl.b])
                            for ct in range(8):
                                mm(pb[4][0:64, ct * 2:(ct + 1) * 2], RK[:, ct, csl], hsel[:], True, True, [RK.b, hsel.b], [pb[4].b])
                            cp("dve", gs[2][:], pb[4][0:64, 0:16], [pb[4].b], [gs[2].b])
                            for half in range(2):
                                tt("dve", y23[:, half * 8:(half + 1) * 8, :], Vtf[half][:].rearrange("p (h k) -> p h k", k=64), gs[2][:, half * 8:(half + 1) * 8].unsqueeze(2).to_broadcast([64, 8, 64]),
                                   ALU.mult, [Vtf[half].b, gs[2].b], [y2.b])
                            tt("dve", yab[:], ybl[:], y2[:], ALU.add, [ybl.b, y2.b], [yab.b])
                            for ct in range(8):
                                mm(pb[5][:, ct * 64:(ct + 1) * 64], yab[:, ct * 128:(ct + 1) * 128], ident[0:64, 0:64], True, True, [yab.b, ident.b], [pb[5].b])
                            tt("dve", mixt[:, :, csl], pb[5][:, :].rearrange("p (c k) -> p c k", k=64), SG[:, :, csl], ALU.mult, [pb[5].b, SG.b], [mixt.b])
                    if full and own:
                        dma("pool", MIX[0:1024, j * TT:(j + 1) * TT].rearrange("(c p) n -> p c n", p=128), mixt[:], [mixt.b], [bMIX])
                        def nat_consumer(cc, z):
                            if cc < 42:
                                cp("act", natq[:], z[:, 1:TT + 1], [z.b], [natq.b])
                                dma("pool", QT[(cc - 34) * 128:(cc - 33) * 128, j * TT:(j + 1) * TT], natq[:], [natq.b], [bQT])
                            elif cc < 50:
                                cp("act", natq[:], z[:, 1:TT + 1], [z.b], [natq.b])
                                dma("pool", KTn[(cc - 42) * 128:(cc - 41) * 128, 256 + j * TT:256 + (j + 1) * TT], natq[:], [natq.b], [bKTn])
                            else:
                                act(natq[:], z[:, 1:TT + 1], AF.Silu, [z.b], [natq.b])
                                dma("pool", GT[(cc - 58) * 128:(cc - 57) * 128, j * TT:(j + 1) * TT], natq[:], [natq.b], [bGT])
                        zchunks(list(range(34, 50)), nat_consumer, False)
                        zchunks(list(range(58, 66)), nat_consumer, False)
                        vtok_tile(xb, rs, 1, TT, 256 + j * TT)

            def vtok_tile(xb_, rs_, c0, ntok, ext0):
                nb = (ntok + 127) // 128
                for tb in range(nb):
                    bs = min(128, ntok - tb * 128)
                    mm(pb[2][0:bs, tb:tb + 1], rs_[0:1, c0 + tb * 128:c0 + tb * 128 + bs], onef[0:1, 0:1], True, True, [rs_.b, onef.b], [pb[2].b])
                cp("dve", rtk[:, 0:nb], pb[2][:, 0:nb], [pb[2].b], [rtk.b])
                for cg in range(4):
                    w = wt[wcount[0] % 2]
                    wcount[0] += 1
                    load_w_group(w, wi_bf, (50 + cg * 2) * 128, 256, bwi)
                    for tb in range(nb):
                        bs = min(128, ntok - tb * 128)
                        pbx = pb[tb % 2]
                        for kc in range(KC):
                            mm(pbx[0:bs, 0:256], xb_[:, kc, c0 + tb * 128:c0 + tb * 128 + bs], w[:, kc, :], kc == 0, kc == KC - 1, [xb_.b, w.b], [pbx.b])
                        act(vtok[0:bs, 0:256], pbx[0:bs, 0:256], AF.Copy, [pbx.b, rtk.b], [vtok.b], scale=rtk[0:bs, tb:tb + 1])
                        dma("pool", VN[ext0 + tb * 128:ext0 + tb * 128 + bs, cg * 256:(cg + 1) * 256], vtok[0:bs, 0:256], [vtok.b], [bVN])

            for u in range(3):
                if str(u) not in UNITS or 'r' not in DO:
                    continue
                tsc("dve", Hm[:], Hm[:], fl[:, u:u + 1], None, ALU.mult, None, [Hm.b, fl.b], [Hm.b])
                cp("act", Hb[:], Hm[:], [Hm.b], [Hb.b])
                run_unit(u, False, False)
                dma("pool", HSd[u], Hm[:].rearrange("p a b -> p (a b)"), [Hm.b], [bHS[u]])

            def init_state(c0):
                hm2 = Hm[:].rearrange("p a b -> p (a b)")
                mset("dve", Hm[:], 0.0, [Hm.b])
                for u in range(3):
                    dma("sp", w2f[:], HSd[u], [bHS[u]], [w2f.b])
                    stt("dve", hm2, w2f[:], fl[:, c0 + u:c0 + u + 1], hm2, ALU.mult, ALU.add, [w2f.b, fl.b, Hm.b], [Hm.b])
                cp("act", Hb[:], Hm[:], [Hm.b], [Hb.b])

            if 'r' in DO and '4' in UNITS:
                init_state(8)
                run_unit(4, True, False)
            if 'r' in DO and '3' in UNITS:
                init_state(5)
                run_unit(3, True, True)
            for hb_ in range(2 if ('r' in DO and 'h' in UNITS) else 0):
                load_x_tile(xbufs, xnh[:, hb_ * 256:(hb_ + 1) * 256], 256)
                ext0 = 0 if hb_ == 0 else 256 + TS
                for g in range(4):
                    w = wt[wcount[0] % 2]
                    wcount[0] += 1
                    load_w_group(w, wi_bf, (42 + g * 2) * 128, 256, bwi)
                    for li in range(2):
                        cc = 42 + g * 2 + li
                        pbm = pb[cc % 2]
                        for kc in range(KC):
                            mm(pbm[:, 0:256], w[:, kc, li * 128:(li + 1) * 128], xb[:, kc, 0:256], kc == 0, kc == KC - 1, [w.b, xb.b], [pbm.b])
                        tt("dve", natq[:, 0:256], pbm[:, 0:256], rs[:, 0:256], ALU.mult, [pbm.b, rs.b], [natq.b])
                        dma("pool", KTn[(cc - 42) * 128:(cc - 41) * 128, ext0:ext0 + 256], natq[:, 0:256], [natq.b], [bKTn])
                vtok_tile(xb, rs, 0, 256, ext0)
        S.barrier()

        with contextlib.ExitStack() as st:
            EBi = sb(st, "EBi", [128, 16, 256], BF16)
            EBe = sb(st, "EBe", [128, 16, 384], BF16)
            ebl = sb(st, "ebl", [128, 16, 384])
            KTs = sb(st, "KTs", [128, 8, 960], BF16)
            QTs = sb(st, "QTs", [128, 8, 512], BF16)
            GTs = sb(st, "GTs", [128, 8, 512], BF16)
            Ve = sb(st, "Ve", [128, 8, 1024], BF16)
            Vo = sb(st, "Vo", [128, 7, 1024], BF16)
            pe_ = [sb(st, "pe%d" % i, [128, 768], BF16) for i in range(2)]
            pp_ = [sb(st, "pp%d" % i, [128, 768], BF16) for i in range(2)]
            rec = sb(st, "rec", [64, 16])
            osb = sb(st, "osb", [64, 1024], BF16)
            mixb = sb(st, "mixb", [128, 8, 512], BF16)
            dma("sp", ebl[:, :, 0:256], bint, [], [ebl.b])
            act(EBi[:], ebl[:, :, 0:256], AF.Exp, [ebl.b], [EBi.b])
            for rb in range(RPS // 8 if 'n' in DO else 0):
                e0 = rb * 512
                dma("sp", KTs[:], KTn[:, e0:e0 + 960].rearrange("(c p) n -> p c n", p=128), [bKTn], [KTs.b])
                dma("sp", QTs[:], QT[:, rb * 512:(rb + 1) * 512].rearrange("(c p) n -> p c n", p=128), [bQT], [QTs.b])
                dma("sp", GTs[:], GT[:, rb * 512:(rb + 1) * 512].rearrange("(c p) n -> p c n", p=128), [bGT], [GTs.b])
                dma("sp", Ve[:], VN[e0:e0 + 1024, :].rearrange("(r p) n -> p r n", p=128), [bVN], [Ve.b])
                dma("sp", Vo[:], VN[e0 + 64:e0 + 64 + 896, :].rearrange("(r p) n -> p r n", p=128), [bVN], [Vo.b])
                for rl in range(8):
                    r = rb * 8 + rl
                    if r < 4:
                        ks, nck, edge = 0, 6, r
                    elif r >= RPS - 3:
                        ks, nck, edge = RPS - 5, 6, 4 + (r - (RPS - 3))
                    else:
                        ks, nck, edge = r, 4, None
                    if edge is not None:
                        dma("sp", ebl[:], bedge[edge], [], [ebl.b])
                        act(EBe[:], ebl[:], AF.Exp, [ebl.b], [EBe.b])
                        EB = EBe
                    else:
                        EB = EBi
                    kl = ks - rb * 8
                    W_ = nck * 64
                    for hp in range(8):
                        pe, pp = pe_[hp % 2], pp_[hp % 2]
                        for e in range(2):
                            ps_ = slice(e * 64, e * 64 + 64)
                            bb = pb[(hp % 2) * 2 + e]
                            for ck in range(nck):
                                mm(bb[:, ck * 64:(ck + 1) * 64], KTs[ps_, hp, (kl + 2 * ck) * 64:(kl + 2 * ck) * 64 + 128], QTs[ps_, hp, rl * 64:(rl + 1) * 64], True, True, [KTs.b, QTs.b], [bb.b])
                            act(pe[:, e * W_:(e + 1) * W_], bb[:, 0:W_], AF.Exp, [bb.b], [pe.b], scale=0.125)
                        tt("dve", pp[:, 0:2 * W_].rearrange("p (e w) -> p e w", e=2), pe[:, 0:2 * W_].rearrange("p (e w) -> p e w", e=2),
                           EB[:, hp * 2:hp * 2 + 2, 0:W_], ALU.mult, [pe.b, EB.b], [pp.b])
                        for e in range(2):
                            h = hp * 2 + e
                            for ck in range(nck):
                                row = kl + 2 * ck
                                if row % 2 == 0:
                                    vv = Ve[:, row // 2, h * 64:(h + 1) * 64]
                                    vb_ = Ve.b
                                else:
                                    vv = Vo[:, (row - 1) // 2, h * 64:(h + 1) * 64]
                                    vb_ = Vo.b
                                lt = pp[:, e * W_ + ck * 64:e * W_ + (ck + 1) * 64]
                                mm(pb[4 + h // 8][0:64, (h % 8) * 64:(h % 8 + 1) * 64], lt, vv, ck == 0, ck == nck - 1, [pp.b, vb_], [pb[4 + h // 8].b])
                                mm(pb[6][0:64, h:h + 1], lt, onesb[:, 0:1], ck == 0, ck == nck - 1, [pp.b, onesb.b], [pb[6].b])
                    S.op("dve", lambda: V.reciprocal(rec[:], pb[6][0:64, 0:16]), [pb[6].b], [rec.b])
                    for half in range(2):
                        tt("dve", osb[:, half * 512:(half + 1) * 512].rearrange("p (h k) -> p h k", k=64), pb[4 + half][0:64, :].rearrange("p (h k) -> p h k", k=64),
                           rec[:, half * 8:(half + 1) * 8].unsqueeze(2).to_broadcast([64, 8, 64]), ALU.mult, [pb[4 + half].b, rec.b], [osb.b])
                    for ct in range(8):
                        mm(pb[7][:, ct * 64:(ct + 1) * 64], osb[:, ct * 128:(ct + 1) * 128], ident[0:64, 0:64], True, True, [osb.b, ident.b], [pb[7].b])
                    tt("dve", mixb[:, :, rl * 64:(rl + 1) * 64], pb[7][:, :].rearrange("p (c k) -> p c k", k=64), GTs[:, :, rl * 64:(rl + 1) * 64], ALU.mult, [pb[7].b, GTs.b], [mixb.b])
                dma("pool", MIX[1024:2048, rb * 512:(rb + 1) * 512].rearrange("(c p) n -> p c n", p=128), mixb[:], [mixb.b], [bMIX])
        S.barrier()

        with contextlib.ExitStack() as st:
            mt = sb(st, "mt", [128, KC, TP], BF16)
            xr = sb(st, "xr", [128, KC, TP])
            Hh = sb(st, "Hh", [128, KC, TP])
            hb2 = sb(st, "hb2", [128, KC, TP], BF16)
            sq2 = sb(st, "sq2", [128, KC, TP], BF16)
            rs1 = sb(st, "rs1", [128, TP])
            rs2 = sb(st, "rs2", [128, TP])
            wA = [sb(st, "wA%d" % i, [128, KC, 512], BF16) for i in range(2)]
            wP = sb(st, "wP", [128, 2, D], BF16)
            ptf = sb(st, "ptf", [128, 2, TP])
            ptb = sb(st, "ptb", [128, 2, TP], BF16)
            gt_ = [sb(st, "gt%d" % i, [128, TP]) for i in range(2)]
            ot = [sb(st, "ot%d" % i, [128, TP]) for i in range(2)]
            dma("sp", wP[:], wp_bf.rearrange("(kc p) n -> p kc n", p=128), [bwp], [wP.b])
            wc2 = 0
            for j in range(TS // TP if 'p' in DO else 0):
                tsl = slice(j * TP, (j + 1) * TP)
                dma("sp", mt[:], MIX[:, tsl].rearrange("(kc p) n -> p kc n", p=128), [bMIX], [mt.b])
                dma("sp", xr[:], xu[3][:, 1 + j * TP:1 + (j + 1) * TP].rearrange("(kc p) n -> p kc n", p=128), [], [xr.b])
                dma("sp", ptf[:], pT[:, tsl].rearrange("(kc p) n -> p kc n", p=128), [], [ptf.b])
                cp("act", ptb[:], ptf[:], [ptf.b], [ptb.b])
                for g in range(4):
                    w = wA[wc2 % 2]
                    wc2 += 1
                    dma("sp", w[:], wo_bf.rearrange("(kc p) n -> p kc n", p=128)[:, :, g * 512:(g + 1) * 512], [bwo], [w.b])
                    for li in range(4):
                        dc = g * 4 + li
                        pbx = pb[dc % 2]
                        for kc in range(KC):
                            mm(pbx[:, :], w[:, kc, li * 128:(li + 1) * 128], mt[:, kc, :], kc == 0, kc == KC - 1, [w.b, mt.b], [pbx.b])
                        tt("dve", Hh[:, dc, :], pbx[:, :], xr[:, dc, :], ALU.add, [pbx.b, xr.b], [Hh.b])
                        act(sq2[:, dc, :], Hh[:, dc, :], AF.Square, [Hh.b], [sq2.b])
                        cp("dve", hb2[:, dc, :], Hh[:, dc, :], [Hh.b], [hb2.b])
                for kc in range(KC):
                    mm(pb[7][:, :], onesb[:], sq2[:, kc, :], kc == 0, kc == KC - 1, [onesb.b, sq2.b], [pb[7].b])
                act(rs1[:], pb[7][:, :], AF.Sqrt, [pb[7].b, epsc.b], [rs1.b], bias=epsc[:, 0:1], scale=1.0 / D)
                S.op("dve", lambda: V.reciprocal(rs1[:], rs1[:]), [rs1.b], [rs1.b])
                for g in range(4):
                    w = wA[wc2 % 2]
                    wc2 += 1
                    dma("sp", w[:], wg_bf.rearrange("(kc p) n -> p kc n", p=128)[:, :, g * 512:(g + 1) * 512], [bwg], [w.b])
                    for li in range(4):
                        dc = g * 4 + li
                        pbx = pb[dc % 2]
                        pby = pb[2 + dc % 2]
                        gt = gt_[dc % 2]
                        for kc in range(KC):
                            mm(pbx[:, :], w[:, kc, li * 128:(li + 1) * 128], hb2[:, kc, :], kc == 0, kc == KC - 1, [w.b, hb2.b], [pbx.b])
                        tt("dve", gt[:], pbx[:, :], rs1[:], ALU.mult, [pbx.b, rs1.b], [gt.b])
                        act(gt[:], gt[:], AF.Sigmoid, [gt.b], [gt.b])
                        for kc in range(2):
                            mm(pby[:, :], wP[:, kc, dc * 128:(dc + 1) * 128], ptb[:, kc, :], kc == 0, kc == 1, [wP.b, ptb.b], [pby.b])
                        tt("dve", gt[:], gt[:], pby[:, :], ALU.mult, [gt.b, pby.b], [gt.b])
                        tt("dve", Hh[:, dc, :], Hh[:, dc, :], gt[:], ALU.add, [Hh.b, gt.b], [Hh.b])
                        act(sq2[:, dc, :], Hh[:, dc, :], AF.Square, [Hh.b], [sq2.b])
                for kc in range(KC):
                    mm(pb[6][:, :], onesb[:], sq2[:, kc, :], kc == 0, kc == KC - 1, [onesb.b, sq2.b], [pb[6].b])
                act(rs2[:], pb[6][:, :], AF.Sqrt, [pb[6].b, epsc.b], [rs2.b], bias=epsc[:, 0:1], scale=1.0 / D)
                S.op("dve", lambda: V.reciprocal(rs2[:], rs2[:]), [rs2.b], [rs2.b])
                for dc in range(KC):
                    o = ot[dc % 2]
                    stt("dve", o[:], Hh[:, dc, :], gv[:, 2, dc:dc + 1], rs2[:], ALU.mult, ALU.mult, [Hh.b, gv.b, rs2.b], [o.b])
                    out_stores.append(dma("pool", outT[dc * 128:(dc + 1) * 128, tsl], o[:], [o.b], []))
        S.finish(out_stores)
        S.emit(st0)
    return nc


def _nat_table(rpb, R, lr0, nck, ROWS):
    kp = np.arange(128)
    ck = np.arange(nck)
    kr = lr0 + 2 * ck[None, :] + (kp[:, None] // 64)
    kc = (kp % 64)[:, None, None]
    qc = np.arange(64)[None, None, :]
    rs_ = min(max(R - 4, 0), ROWS - 8)
    cs = np.clip(qc - 8, 0, 64 - 16)
    rowok = (kr >= rs_) & (kr < rs_ + 8)
    colok = (kc >= cs) & (kc < cs + 16)
    ok = rowok[:, :, None] & colok
    dr = np.clip(kr - R + 7, 0, 14)[:, :, None]
    dc = np.clip(kc - qc + 15, 0, 30)
    dr_b = np.broadcast_to(dr, ok.shape)
    dc_b = np.broadcast_to(dc, ok.shape)
    tab = np.empty((128, 16, nck, 64), np.float32)
    for h in range(16):
        tab[:, h] = np.where(ok, rpb[h][dr_b, dc_b], np.float32(NEG))
    return tab.reshape(128, 16, nck * 64)


_CACHE = {}


def kernel(x, p, norm_mix_g, w_in, shift_mu_prev, shift_mu_next, decay_w0, decay_w2, iclr_a0, iclr_a2,
           k_k, k_a, r_k, lnx_w, lnx_b, nat_rpb, w_out, ple_norm_g, w_ple_gate, w_ple_proj, final_norm_g, _debug=False):
    f = lambda a: np.ascontiguousarray(np.asarray(a, dtype=np.float32))
    x = f(x); p = f(p)
    B, SEQ, _ = x.shape
    TS = SEQ // 4
    ROWS = SEQ // 64
    RPS = TS // 64
    key = (TS, _debug)
    if key not in _CACHE:
        _CACHE[key] = build_program(TS, _debug)
    nc = _CACHE[key]
    mup, mun = f(shift_mu_prev)[0], f(shift_mu_next)[0]
    w0_, w2_, a0_, a2_ = f(decay_w0)[0], f(decay_w2)[0], f(iclr_a0)[0], f(iclr_a2)[0]
    col = lambda v, n: np.ascontiguousarray(v.reshape(n, 128).T)
    gvec = np.stack([col(f(norm_mix_g)[0], 16), col(f(ple_norm_g)[0], 16), col(f(final_norm_g), 16)], axis=1)
    kvec = np.stack([col(f(k_k)[0], 8), col(f(k_a)[0], 8), col(f(r_k)[0].reshape(-1), 8)], axis=1)
    lnx = np.stack([f(lnx_w)[0], f(lnx_b)[0]], axis=0)
    rpb = f(nat_rpb)[0]
    common = {"w_in": f(w_in)[0], "w_out": f(w_out)[0], "w_gate": f(w_ple_gate)[0], "w_ple": f(w_ple_proj)[0],
              "gvec": np.ascontiguousarray(gvec), "kvec": np.ascontiguousarray(kvec), "lnx": lnx}
    in_maps = []
    for c in range(8):
        b, s = c // 4, c % 4
        xT = np.zeros((D, SEQ + 2), np.float32)
        xT[:, 1:SEQ + 1] = x[b].T
        units = [(i, False) for i in range(s)] + [(i, True) for i in range(3, s, -1)]
        keep = [0.0 if (ui == 0 or ui == s) else 1.0 for ui in range(3)]
        units = units + [(s, False), (s, True)]
        xu = np.empty((5, D, TS + 2), np.float32)
        muv = np.empty((128, 2, 5, 26), np.float32)
        w0a0 = np.empty((128, 2, 5, 8), np.float32)
        w2u = np.zeros((5, 128, 1024), np.float32)
        a2u = np.zeros((5, 128, 1024), np.float32)
        for ui, (seg, rev) in enumerate(units):
            blk = xT[:, seg * TS:seg * TS + TS + 2]
            xu[ui] = blk[:, ::-1] if rev else blk
            d_ = 1 if rev else 0
            muv[:, 0, ui, :] = col(mun if rev else mup, 26)
            muv[:, 1, ui, :] = col(mup if rev else mun, 26)
            w0a0[:, 0, ui, :] = col(w0_[d_], 8)
            w0a0[:, 1, ui, :] = col(a0_[d_], 8)
            w2u[ui, d_ * 64:(d_ + 1) * 64, :] = w2_[d_]
            a2u[ui, d_ * 64:(d_ + 1) * 64, :] = a2_[d_]
        flags = np.zeros((128, 16), np.float32)
        flags[:, 0:3] = np.asarray(keep, np.float32)[None, :]
        if s > 0:
            flags[:, 5 + s - 1] = 1.0
        if s < 3:
            flags[:, 8 + 2] = 1.0
        xnh = np.zeros((D, 512), np.float32)
        t0 = s * TS
        lo = max(t0 - 256, 0)
        xnh[:, 256 - (t0 - lo):256] = x[b, lo:t0].T
        hi = min(t0 + TS + 256, SEQ)
        xnh[:, 256:256 + (hi - (t0 + TS))] = x[b, t0 + TS:hi].T
        R0 = s * RPS
        bint = _nat_table(rpb, ROWS // 2, ROWS // 2 - 4, 4, ROWS)
        bedge = np.empty((7, 128, 16, 384), np.float32)
        for e in range(4):
            bedge[e] = _nat_table(rpb, R0 + e, R0 - 4, 6, ROWS)
        for e in range(3):
            bedge[4 + e] = _nat_table(rpb, R0 + RPS - 3 + e, R0 + RPS - 9, 6, ROWS)
        m = dict(common)
        m.update({"xu": xu, "xnh": xnh, "pT": np.ascontiguousarray(p[0, b, t0:t0 + TS].T), "muv": muv, "w0a0": w0a0,
                  "w2u": w2u, "a2u": a2u, "flags": flags, "bint": bint, "bedge": bedge})
        in_maps.append(m)
    res = run_bass_kernel_spmd(nc, in_maps, core_ids=list(range(8)))
    out = np.empty((B, SEQ, D), np.float32)
    for c in range(8):
        b, s = c // 4, c % 4
        out[b, s * TS:(s + 1) * TS, :] = np.asarray(res.results[c]["outT"]).T
    if _debug:
        return out, res
    return out
```
